# Optimizing a Trainium2 kernel written in Bass

```python
import jax
import jax.numpy as jnp
from jax import lax
import numpy as np


D_MODEL = 2048
BATCH = 1
SEQ = 16384
DEPTH = 1

CHUNK = 64
Q_BLOCK = 128
ROPE_THETA = 10000.0
EPS = 1e-6
MIX_WIDTH = D_MODEL
HEAD_DIM = 128
A_WIDTH = MIX_WIDTH // 2
B_WIDTH = MIX_WIDTH - A_WIDTH
A_HEADS = A_WIDTH // HEAD_DIM
B_HEADS = B_WIDTH // HEAD_DIM
IDX_HEADS = 16
IDX_DIM = 64
TOPK_MAX = 256
SPLITS = (A_WIDTH, A_WIDTH, A_WIDTH, A_WIDTH, IDX_HEADS * IDX_DIM, IDX_DIM, IDX_HEADS,
          B_WIDTH, B_WIDTH, B_WIDTH, B_WIDTH)
N_IN = sum(SPLITS)

kernel_name = 'hybrid_dsa_hgrn2_adaln_block'


def rms_norm(t, g):
    tf = t.astype(jnp.float32)
    y = tf * lax.rsqrt(jnp.mean(tf * tf, axis=-1, keepdims=True) + EPS)
    return (y * g.astype(jnp.float32)).astype(t.dtype)


def rope(t, pos):
    half = t.shape[-1] // 2
    inv_freq = ROPE_THETA ** (-jnp.arange(half, dtype=jnp.float32) / half)
    ang = pos.astype(jnp.float32)[:, :, None, None] * inv_freq
    cos, sin = jnp.cos(ang), jnp.sin(ang)
    tf = t.astype(jnp.float32)
    t1, t2 = tf[..., :half], tf[..., half:]
    return jnp.concatenate([t1 * cos - t2 * sin, t2 * cos + t1 * sin], axis=-1).astype(t.dtype)


def dsa_sparse_attention(q, k, v, q_idx, k_idx, w_idx):
    bsz, seq, n_heads, head_dim = q.shape
    n_sel = min(TOPK_MAX, seq // 4)
    key_pos = jnp.arange(seq)
    idx_scale = IDX_DIM ** -0.5
    att_scale = head_dim ** -0.5
    k_idx32 = k_idx.astype(jnp.float32)

    def one_block(blk):
        start = blk * Q_BLOCK
        qb = lax.dynamic_slice_in_dim(q, start, Q_BLOCK, axis=1)
        qib = lax.dynamic_slice_in_dim(q_idx, start, Q_BLOCK, axis=1).astype(jnp.float32)
        wb = lax.dynamic_slice_in_dim(w_idx, start, Q_BLOCK, axis=1).astype(jnp.float32)
        q_pos = start + jnp.arange(Q_BLOCK)
        limit = (q_pos // CHUNK + 1) * CHUNK
        admissible = key_pos[None, :] < limit[:, None]
        logits = jnp.einsum('bqhd,bsd->bqhs', qib, k_idx32) * idx_scale
        score = jnp.einsum('bqhs,bqh->bqs', jax.nn.relu(logits), wb)
        score = jnp.where(admissible[None], score, -jnp.inf)
        _, sel = lax.top_k(score, n_sel)
        sel_ok = sel < limit[None, :, None]
        kg = jax.vmap(lambda kk, ii: kk[ii])(k, sel)
        vg = jax.vmap(lambda vv, ii: vv[ii])(v, sel)
        s = jnp.einsum('bqhd,bqnhd->bqhn', qb, kg).astype(jnp.float32) * att_scale
        s = jnp.where(sel_ok[:, :, None, :], s, -jnp.inf)
        p = jax.nn.softmax(s, axis=-1)
        return jnp.einsum('bqhn,bqnhd->bqhd', p.astype(v.dtype), vg)

    out = lax.map(one_block, jnp.arange(seq // Q_BLOCK))
    return jnp.moveaxis(out, 0, 1).reshape(bsz, seq, n_heads, head_dim)


def hgrn2_recurrence(q, f_pre, i_in, lower_bound):
    bsz, seq, n_heads, d = q.shape
    n_chunks = seq // CHUNK

    def to_chunks(t):
        return t.astype(jnp.float32).reshape(bsz, n_chunks, CHUNK, n_heads, d).transpose(0, 3, 1, 2, 4)

    lb = lower_bound.astype(jnp.float32)[None, :, None, None, :]
    qc = jax.nn.silu(to_chunks(q))
    forget = lb + (1.0 - lb) * jax.nn.sigmoid(to_chunks(f_pre))
    kc = 1.0 - forget
    vc = to_chunks(i_in)
    b = jnp.cumsum(jnp.log(forget), axis=3)
    b_mid = b[:, :, :, CHUNK // 2 - 1:CHUNK // 2, :]
    b_last = b[:, :, :, -1:, :]
    causal = jnp.tril(jnp.ones((CHUNK, CHUNK), dtype=bool))
    a = jnp.einsum('bhncd,bhnsd->bhncs', qc * jnp.exp(b - b_mid), kc * jnp.exp(b_mid - b))
    a = jnp.where(causal, a, 0.0)
    o_intra = jnp.einsum('bhncs,bhnse->bhnce', a, vc)
    chunk_state = jnp.einsum('bhnsd,bhnse->bhnde', kc * jnp.exp(b_last - b), vc)
    chunk_decay = jnp.exp(b_last[:, :, :, 0, :])

    def step(state, inp):
        ds, dec = inp
        return state * dec[..., None] + ds, state

    _, prev = lax.scan(step, jnp.zeros((bsz, n_heads, d, d), jnp.float32),
                       (jnp.moveaxis(chunk_state, 2, 0), jnp.moveaxis(chunk_decay, 2, 0)))
    prev = jnp.moveaxis(prev, 0, 2)
    o_inter = jnp.einsum('bhncd,bhnde->bhnce', qc * jnp.exp(b), prev)
    o = (o_intra + o_inter).transpose(0, 2, 3, 1, 4).reshape(bsz, seq, n_heads, d)
    return o.astype(q.dtype)


def setup_inputs(seed: int = 0) -> dict:
    key = jax.random.key(seed)
    ks = jax.random.split(key, 14)
    x = jax.random.normal(ks[0], (BATCH, SEQ, D_MODEL), jnp.float32)
    c = jax.random.normal(ks[1], (BATCH, D_MODEL), jnp.float32)
    offset = jax.random.randint(ks[2], (BATCH, 1), 0, 4096, dtype=jnp.int32)
    positions = (offset + jnp.arange(SEQ, dtype=jnp.int32)[None, :]).astype(jnp.int32)
    ada_w = jax.random.normal(ks[3], (DEPTH, D_MODEL, 3 * D_MODEL), jnp.float32) * (0.5 * D_MODEL ** -0.5)
    ada_b = 0.02 * jax.random.normal(ks[4], (DEPTH, 3 * D_MODEL), jnp.float32)
    norm_g = 1.0 + 0.02 * jax.random.normal(ks[5], (DEPTH, D_MODEL), jnp.float32)
    w_in = jax.random.normal(ks[6], (DEPTH, D_MODEL, N_IN), jnp.float32) * D_MODEL ** -0.5
    q_norm_g = 1.0 + 0.02 * jax.random.normal(ks[7], (DEPTH, HEAD_DIM), jnp.float32)
    k_norm_g = 1.0 + 0.02 * jax.random.normal(ks[8], (DEPTH, HEAD_DIM), jnp.float32)
    idx_k_norm_g = 1.0 + 0.02 * jax.random.normal(ks[9], (DEPTH, IDX_DIM), jnp.float32)
    hgrn_lb_logits = 0.1 * jax.random.normal(ks[10], (DEPTH + 1, B_WIDTH), jnp.float32)
    hgrn_norm_g = 1.0 + 0.02 * jax.random.normal(ks[11], (DEPTH, HEAD_DIM), jnp.float32)
    w_out = jax.random.normal(ks[12], (DEPTH, MIX_WIDTH, D_MODEL), jnp.float32) * MIX_WIDTH ** -0.5
    return {'x': x, 'c': c, 'positions': positions, 'ada_w': ada_w, 'ada_b': ada_b,
            'norm_g': norm_g, 'w_in': w_in, 'q_norm_g': q_norm_g, 'k_norm_g': k_norm_g,
            'idx_k_norm_g': idx_k_norm_g, 'hgrn_lb_logits': hgrn_lb_logits,
            'hgrn_norm_g': hgrn_norm_g, 'w_out': w_out}


def reference(x, c, positions, ada_w, ada_b, norm_g, w_in, q_norm_g, k_norm_g, idx_k_norm_g,
              hgrn_lb_logits, hgrn_norm_g, w_out):
    bsz, seq, _ = x.shape
    lower_bounds = jnp.cumsum(jax.nn.softmax(hgrn_lb_logits.astype(jnp.float32), axis=0), axis=0)
    offsets = np.cumsum(SPLITS)[:-1].tolist()
    for layer in range(DEPTH):
        mod = jax.nn.silu(c) @ ada_w[layer] + ada_b[layer]
        shift, scale, gate = jnp.split(mod, 3, axis=-1)
        h = rms_norm(x, norm_g[layer]) * (1.0 + scale[:, None, :]) + shift[:, None, :]
        proj = h @ w_in[layer]
        a_q, a_k, a_v, a_g, i_q, i_k, i_w, r_q, r_f, r_i, r_g = jnp.split(proj, offsets, axis=-1)

        q = rope(rms_norm(a_q.reshape(bsz, seq, A_HEADS, HEAD_DIM), q_norm_g[layer]), positions)
        k = rope(rms_norm(a_k.reshape(bsz, seq, A_HEADS, HEAD_DIM), k_norm_g[layer]), positions)
        v = a_v.reshape(bsz, seq, A_HEADS, HEAD_DIM)
        qi = rope(i_q.reshape(bsz, seq, IDX_HEADS, IDX_DIM), positions)
        ki = rope(rms_norm(i_k, idx_k_norm_g[layer])[:, :, None, :], positions)[:, :, 0, :]
        wi = i_w * (IDX_HEADS ** -0.5)
        o_a = dsa_sparse_attention(q, k, v, qi, ki, wi).reshape(bsz, seq, A_WIDTH) * jax.nn.silu(a_g)

        lb = lower_bounds[layer].reshape(B_HEADS, HEAD_DIM)
        o_r = hgrn2_recurrence(r_q.reshape(bsz, seq, B_HEADS, HEAD_DIM),
                               r_f.reshape(bsz, seq, B_HEADS, HEAD_DIM),
                               r_i.reshape(bsz, seq, B_HEADS, HEAD_DIM), lb)
        o_r = rms_norm(o_r, hgrn_norm_g[layer]).reshape(bsz, seq, B_WIDTH) * jax.nn.silu(r_g)

        mix = jnp.concatenate([o_a, o_r], axis=-1) @ w_out[layer]
        x = x + gate[:, None, :] * mix
    return x
```

```python
import numpy as np
import ml_dtypes
import concourse.bass as bass
import concourse.mybir as mybir
from concourse.bass_utils import run_bass_kernel_spmd

F32 = mybir.dt.float32
BF16 = mybir.dt.bfloat16
I32 = mybir.dt.int32
U8 = mybir.dt.uint8
AF = mybir.ActivationFunctionType
ALU = mybir.AluOpType
AX = mybir.AxisListType

D = 2048
NIN = 9296
KC = 16
EPS = 1e-6
NIT = 14
TOPK = 256
BIG = 1.0e30
PI = float(np.pi)
C1 = 6.28125
C2 = float(2.0 * np.pi - 6.28125)
O_AQ, O_AK, O_AV, O_AG, O_IQ, O_IK, O_IW, O_RQ, O_RF, O_RI, O_RG = (
    0, 1024, 2048, 3072, 4096, 5120, 5184, 5200, 6224, 7248, 8272)


class Buf:
    def __init__(self, name):
        self.name = name
        self.w = {}
        self.r = {}
        self.dsem = None
        self.dcnt = 0
        self.noww = False


class T:
    def __init__(self, t, name, b=None):
        self.t = t
        self.b = b if b is not None else Buf(name)

    def __getitem__(self, k):
        return self.t[k]

    def view(self, ap):
        return T(ap, self.b.name, self.b)


class _Rec:
    def __init__(self):
        self.call = None

    def __getattr__(self, name):
        def f(*a, **kw):
            self.call = (name, a, kw)
            return self
        return f


class Eng:
    def __init__(self, name, sem, is_pe=False, is_q=False):
        self.name = name
        self.sem = sem
        self.cnt = 0
        self.known = {}
        self.q = []
        self.is_pe = is_pe


class Prog:
    def __init__(self, nc):
        self.nc = nc
        self.es = None
        self.stack = None
        self.engs = {}
        self.nsem = 0
        self.dbufs = []

    def new_sem(self, name):
        self.nsem += 1
        return self.stack.enter_context(self.nc.semaphore(name))

    def setup(self):
        for n, pe in (("pe", True), ("act", False), ("dve", False), ("pool", False), ("sp", False)):
            self.engs[n] = Eng(n, self.new_sem("s_" + n), is_pe=pe)

    def _deps(self, e, R, W):
        deps = {}
        for t in list(R) + [w_ for w_ in W if not w_.b.noww]:
            for s, v in t.b.w.items():
                if deps.get(s, 0) < v:
                    deps[s] = v
        for t in W:
            for s, v in t.b.r.items():
                if deps.get(s, 0) < v:
                    deps[s] = v
        for s, v in deps.items():
            if s is e.sem and e.is_pe:
                continue
            if e.known.get(s, 0) < v:
                e.q.append(("w", s, v))
                e.known[s] = v

    def op(self, en, fn, R=(), W=()):
        e = self.engs[en]
        self._deps(e, R, W)
        e.cnt += 1
        rec = _Rec()
        fn(rec)
        e.q.append(("i", rec.call, e.sem, 1))
        ev = (e.sem, e.cnt)
        for t in W:
            t.b.w[ev[0]] = ev[1]
            t.b.r = {}
        for t in R:
            t.b.r[ev[0]] = ev[1]

    def dma(self, en, out, in_, sb, R=(), W=()):
        e = self.engs[en]
        self._deps(e, R, W)
        if sb.b.dsem is None:
            sb.b.dsem = self.new_sem("d_" + sb.b.name)
            self.dbufs.append(sb.b)
        sb.b.dcnt += 16
        sem, val = sb.b.dsem, sb.b.dcnt
        e.q.append(("d", out, in_, sem))
        for t in W:
            t.b.w[sem] = val
            t.b.r = {}
        for t in R:
            t.b.r[sem] = val

    def barrier(self):
        evs = [(e.sem, e.cnt) for e in self.engs.values() if e.cnt > 0]
        evs += [(b.dsem, b.dcnt) for b in self.dbufs]
        for e in self.engs.values():
            for s_, v in evs:
                if s_ is e.sem:
                    continue
                if e.known.get(s_, 0) < v:
                    e.q.append(("w", s_, v))
                    e.known[s_] = v

    def wait_all(self, en, ts):
        e = self.engs[en]
        self._deps(e, ts, ts)

    def emit(self, block):
        nc = self.nc
        m = {"pe": block.tensor, "act": block.scalar, "dve": block.vector, "pool": block.gpsimd,
             "sp": block.sync}
        for n, e in self.engs.items():
            def body(eng, e=e):
                for it in e.q:
                    if it[0] == "w":
                        eng.wait_ge(it[1], it[2])
                    elif it[0] == "i":
                        nm, a_, kw_ = it[1]
                        getattr(eng, nm)(*a_, **kw_).then_inc(it[2], it[3])
                    else:
                        eng.dma_start(out=it[1], in_=it[2]).then_inc(it[3], 16)
            m[n](body)
        for e in self.engs.values():
            e.q = []


def build(G):
    import os
    STOP = float(os.environ.get('KSTOP', '99'))
    KSEG = int(os.environ.get('KSEG', '99'))
    KATT = int(os.environ.get('KATT', '1'))
    KIDX = int(os.environ.get('KIDX', '1'))
    NT = 2048 * G + 256
    NTL = NT // 128
    NOWN = 256 * G
    NO = 2 * G
    NCH = G + 1
    NB = NT // 128

    nc = bass.Bass("TRN2", target_bir_lowering=False)
    dt = nc.dram_tensor

    def ext(name, shape, dtype=F32):
        return dt(name, list(shape), dtype, kind="ExternalInput").ap()

    xa = ext("xa", [NT, D])
    posa = ext("posa", [128, NTL], I32)
    valid = ext("valid", [128, NTL])
    vbias_in = ext("vbias", [128, 2048])
    cfm = ext("cfm", [128, KC])
    ada_w = ext("ada_w", [D, 3 * D])
    adab_fm = ext("adab_fm", [128, 48])
    adab_g = ext("adab_g", [1, D])
    ng_fm = ext("ng_fm", [128, KC])
    w_in = ext("w_in", [D, NIN])
    w_out = ext("w_out", [D, D])
    gq_in = ext("gq", [128, 128])
    gk_in = ext("gk", [128, 128])
    gi_in = ext("gi", [128, 64])
    hg_in = ext("hg", [128, 128])
    lbl_in = ext("lbl", [2, 128, 1024])
    invf_in = ext("invf", [128, 64])
    identf_in = ext("identf", [128, 128])
    md1_in = ext("md1", [128, 128])
    md2_in = ext("md2", [128, 128])
    md3_in = ext("md3", [128, 128])
    cind_in = ext("cind", [128, 2])
    cmask_in = ext("cmask", [128, 128])
    cbias_in = ext("cbias", [128, 2, 256])
    y = dt("y", [NOWN, D], F32, kind="ExternalOutput").ap()

    hT_d = dt("hT_d", [NTL, 128, KC * 128], BF16).ap()
    KT_d = dt("KT_d", [8, 128, NT], BF16).ap()
    V_d = dt("V_d", [8, NCH, 128, 16 * 128], BF16).ap()
    KI_d = dt("KI_d", [64, NT], BF16).ap()
    SN_d = dt("SN_d", [G, 128, 8 * 128], F32).ap()
    QT_d = dt("QT_d", [8, 128, NOWN], BF16).ap()
    QI_d = dt("QI_d", [8, 128, NOWN], BF16).ap()
    AG_d = dt("AG_d", [NOWN, 1024], BF16).ap()
    CT_d = dt("CT_d", [KC, 128, NOWN], BF16).ap()

    from contextlib import ExitStack
    P = Prog(nc)
    with ExitStack() as es:
        P.stack = es
        P.setup()

        def flush():
            P.barrier()
            with nc.Block() as block:
                P.emit(block)
        dr = {n: T(None, n) for n in ("hT", "KT", "V", "KI", "SN", "QT", "QI", "AG", "CT", "y")}
        for v_ in dr.values():
            v_.b.noww = True
        cnt = [0]

        def sb(shape, dtype, name=None, stack=None):
            cnt[0] += 1
            nm = f"{name or 't'}_{cnt[0]}"
            return T((stack or es).enter_context(nc.sbuf_tensor(nm, list(shape), dtype)), nm)

        def ps(shape, dtype, name=None, stack=None):
            cnt[0] += 1
            nm = f"{name or 'p'}_{cnt[0]}"
            return T((stack or es).enter_context(nc.psum_tensor(nm, list(shape), dtype)), nm)

        def load_const(src, shape, dtype=F32, name="c", q="sp"):
            t = sb(shape, dtype, name)
            P.dma(q, t[:], src, t, W=[t])
            return t

        identf = load_const(identf_in[:, :], [128, 128], name="identf")
        identb = sb([128, 128], BF16, "identb")
        P.op("dve", lambda e: e.tensor_copy(out=identb[:], in_=identf[:]), R=[identf], W=[identb])
        md1 = load_const(md1_in[:, :], [128, 128], name="md1")
        md2 = load_const(md2_in[:, :], [128, 128], name="md2")
        md3 = load_const(md3_in[:, :], [128, 128], name="md3")
        cind = load_const(cind_in[:, :], [128, 2], name="cind")
        cmask = load_const(cmask_in[:, :], [128, 128], name="cmask")
        invf = load_const(invf_in[:, :], [128, 64], name="invf")
        gq = load_const(gq_in[:, :], [128, 128], name="gq")
        gk = load_const(gk_in[:, :], [128, 128], name="gk")
        gi = load_const(gi_in[:, :], [128, 64], name="gi")
        hg = load_const(hg_in[:, :], [128, 128], name="hg")
        posi = load_const(posa[:, :], [128, NTL], I32, name="posi")
        posf = sb([128, NTL], F32, "posf")
        P.op("dve", lambda e: e.tensor_copy(out=posf[:], in_=posi[:]), R=[posi], W=[posf])
        validt = load_const(valid[:, :], [128, NTL], name="valid")
        Amod = sb([128, KC], F32, "Amod")
        Bmod = sb([128, KC], F32, "Bmod")
        grow = sb([1, D], F32, "grow")
        wsc = sb([128, NO, 16], F32, "wsc")
        ones1 = sb([1, 128], F32, "ones1")
        P.op("dve", lambda e: e.memset(ones1[:], 1.0), W=[ones1])

        s0 = ExitStack()
        if STOP >= 0:
            sc = sb([128, KC, 2], F32, "sc", s0)
            cin = sb([128, KC], F32, "cin", s0)
            P.dma("sp", cin[:], cfm[:, :], cin, W=[cin])
            for i2 in range(2):
                P.op("act", lambda e, i2=i2: e.activation(out=sc[:, :, i2], in_=cin[:], func=AF.Silu), R=[cin], W=[sc])
            adb = sb([128, 48], F32, "adb", s0)
            P.dma("sp", adb[:], adab_fm[:, :], adb, W=[adb])
            adg = sb([1, D], F32, "adg", s0)
            P.dma("sp", adg[:], adab_g[:, :], adg, W=[adg])
            ngf = sb([128, KC], F32, "ngf", s0)
            P.dma("sp", ngf[:], ng_fm[:, :], ngf, W=[ngf])
            awb = [sb([128, KC, 512], F32, "awb", s0) for _ in range(2)]
            mods_b = ps([128, 512], F32, "mods", s0)
            mods = mods_b.view(mods_b[:, 0:64].rearrange("p (c t) -> p c t", t=2))
            gps = ps([128, 512], F32, "gps", s0)
            awv = ada_w.rearrange("(kc p) n -> p kc n", p=128)
            for nb in range(12):
                a = awb[nb % 2]
                for hh in range(2):
                    P.dma("sp", a[:, hh * 8:(hh + 1) * 8, :], awv[:, hh * 8:(hh + 1) * 8, nb * 512:(nb + 1) * 512],
                          a, W=[a])
                if nb < 8:
                    for jj in range(4):
                        col = nb * 4 + jj
                        for kc in range(KC):
                            P.op("pe", lambda e, a=a, col=col, kc=kc, jj=jj: e.matmul(
                                mods[:, col, :], lhsT=a[:, kc, jj * 128:(jj + 1) * 128], rhs=sc[:, kc, :],
                                start=(kc == 0), stop=(kc == KC - 1)), R=[a, sc], W=[mods])
                else:
                    gb = nb - 8
                    for kc in range(KC):
                        P.op("pe", lambda e, a=a, kc=kc: e.matmul(
                            gps[0:1, :], lhsT=sc[:, kc, 0:1], rhs=a[:, kc, :],
                            start=(kc == 0), stop=(kc == KC - 1)), R=[a, sc], W=[gps])
                    P.op("dve", lambda e, gb=gb: e.tensor_tensor(
                        out=grow[0:1, gb * 512:(gb + 1) * 512], in0=gps[0:1, :],
                        in1=adg[0:1, gb * 512:(gb + 1) * 512], op=ALU.add), R=[gps, adg], W=[grow])
            P.op("dve", lambda e: e.tensor_tensor(out=Bmod[:], in0=mods[:, 0:16, 0], in1=adb[:, 0:16], op=ALU.add),
                 R=[mods, adb], W=[Bmod])
            P.op("dve", lambda e: e.tensor_tensor(out=Amod[:], in0=mods[:, 16:32, 0], in1=adb[:, 16:32], op=ALU.add),
                 R=[mods, adb], W=[Amod])
            P.op("dve", lambda e: e.scalar_tensor_tensor(out=Amod[:], in0=Amod[:], scalar=1.0, in1=ngf[:],
                                                        op0=ALU.add, op1=ALU.mult), R=[Amod, ngf], W=[Amod])

        flush(); s0.close()
        SINMAX = 3.1415925

        def evens(ap2d):
            return ap2d.rearrange("p (d two) -> p d two", two=2)[:, :, 0]

        def rope_tables(col, cs_out, tmp, ni):
            ang, nn, t2 = tmp[:, 0, :], tmp[:, 1, :], tmp[:, 2, :]
            pcol = posf[:, col:col + 1]
            P.op("dve", lambda e: e.tensor_scalar(out=ang, in0=invf[:], scalar1=pcol, scalar2=None, op0=ALU.mult),
                 R=[invf, posf], W=[tmp])
            P.op("dve", lambda e: e.tensor_scalar(out=nn, in0=ang, scalar1=1.0 / (2 * PI), scalar2=None, op0=ALU.mult),
                 R=[tmp], W=[tmp])
            P.op("dve", lambda e: e.tensor_copy(out=ni[:], in_=nn), R=[tmp], W=[ni])
            P.op("dve", lambda e: e.tensor_copy(out=nn, in_=ni[:]), R=[ni], W=[tmp])
            P.op("dve", lambda e: e.scalar_tensor_tensor(out=ang, in0=nn, scalar=-C1, in1=ang, op0=ALU.mult, op1=ALU.add),
                 R=[tmp], W=[tmp])
            P.op("dve", lambda e: e.scalar_tensor_tensor(out=ang, in0=nn, scalar=-C2, in1=ang, op0=ALU.mult, op1=ALU.add),
                 R=[tmp], W=[tmp])
            P.op("dve", lambda e: e.tensor_scalar(out=t2, in0=ang, scalar1=PI, scalar2=-2 * PI, op0=ALU.is_gt, op1=ALU.mult),
                 R=[tmp], W=[tmp])
            P.op("dve", lambda e: e.tensor_tensor(out=ang, in0=ang, in1=t2, op=ALU.add), R=[tmp], W=[tmp])
            P.op("dve", lambda e: e.tensor_scalar(out=t2, in0=ang, scalar1=-PI, scalar2=2 * PI, op0=ALU.is_lt, op1=ALU.mult),
                 R=[tmp], W=[tmp])
            P.op("dve", lambda e: e.tensor_tensor(out=ang, in0=ang, in1=t2, op=ALU.add), R=[tmp], W=[tmp])
            P.op("dve", lambda e: e.tensor_scalar(out=ang, in0=ang, scalar1=SINMAX, scalar2=-SINMAX, op0=ALU.min, op1=ALU.max),
                 R=[tmp], W=[tmp])
            P.op("act", lambda e: e.activation(out=cs_out[:, 1, :], in_=ang, func=AF.Sin), R=[tmp], W=[cs_out])
            P.op("act", lambda e: e.activation(out=t2, in_=ang, func=AF.Abs), R=[tmp], W=[tmp])
            P.op("act", lambda e: e.activation(out=cs_out[:, 0, :], in_=t2, func=AF.Sin, bias=halfpi[:, 0:1], scale=-1.0),
                 R=[tmp, halfpi], W=[cs_out])

        def gain_tables(cs, g, tab, half, sub):
            cos, sin = cs[:, 0, :], cs[:, 1, :]
            if sub:
                cos, sin = evens(cos), evens(sin)
            if g is None:
                for i, a in enumerate((cos, sin, cos, sin)):
                    P.op("dve", lambda e, i=i, a=a: e.tensor_copy(out=tab[:, i, :], in_=a), R=[cs], W=[tab])
                return
            g1, g2 = g[:, 0:half], g[:, half:2 * half]
            for i, (a, b_) in enumerate(((cos, g1), (sin, g2), (cos, g2), (sin, g1))):
                P.op("dve", lambda e, i=i, a=a, b_=b_: e.tensor_tensor(out=tab[:, i, :], in0=a, in1=b_, op=ALU.mult),
                     R=[cs, g], W=[tab])

        def rope_apply(eng, src, dst, nh, half, tab, tmp):
            def bc(i):
                return tab[:, i, :].unsqueeze(1).broadcast_to([128, nh, half])
            x1, x2 = src[:, :, 0:half], src[:, :, half:2 * half]
            ta = tmp[:, 0, 0:nh * half].rearrange("p (h d) -> p h d", h=nh)
            tb = tmp[:, 1, 0:nh * half].rearrange("p (h d) -> p h d", h=nh)
            o1, o2 = dst[:, :, 0:half], dst[:, :, half:2 * half]
            P.op(eng, lambda e: e.tensor_tensor(out=ta, in0=x1, in1=bc(0), op=ALU.mult), R=[src, tab], W=[tmp])
            P.op(eng, lambda e: e.tensor_tensor(out=tb, in0=x2, in1=bc(1), op=ALU.mult), R=[src, tab], W=[tmp])
            P.op(eng, lambda e: e.tensor_tensor(out=o1, in0=ta, in1=tb, op=ALU.subtract), R=[tmp], W=[dst])
            P.op(eng, lambda e: e.tensor_tensor(out=ta, in0=x2, in1=bc(2), op=ALU.mult), R=[src, tab], W=[tmp])
            P.op(eng, lambda e: e.tensor_tensor(out=tb, in0=x1, in1=bc(3), op=ALU.mult), R=[src, tab], W=[tmp])
            P.op(eng, lambda e: e.tensor_tensor(out=o2, in0=ta, in1=tb, op=ALU.add), R=[tmp], W=[dst])

        def rstd_of(ss, n, width):
            P.op("act", lambda e: e.activation(out=ss[:, 0:n], in_=ss[:, 0:n], func=AF.Sqrt, bias=epsc[:, 0:1],
                                               scale=1.0 / width), R=[ss, epsc], W=[ss])
            P.op("dve", lambda e: e.reciprocal(out=ss[:, 0:n], in_=ss[:, 0:n]), R=[ss], W=[ss])

        halfpi = sb([128, 1], F32, "halfpi")
        P.op("dve", lambda e: e.memset(halfpi[:], PI / 2 - 1e-6), W=[halfpi])
        epsc = sb([128, 1], F32, "epsc")
        P.op("dve", lambda e: e.memset(epsc[:], EPS), W=[epsc])

        def load_w_cols(dst, dcol, c0, ncols, stg, cast_i):
            for kc in range(KC):
                s = stg[cast_i[0] % len(stg)]
                P.dma("sp", s[:, 0:ncols], w_in[kc * 128:(kc + 1) * 128, c0:c0 + ncols], s, W=[s])
                en = ("pool", "dve", "act")[cast_i[0] % 3]
                if en == "act":
                    P.op("act", lambda e, s=s, kc=kc: e.copy(out=dst[:, kc, dcol:dcol + ncols], in_=s[:, 0:ncols]),
                         R=[s], W=[dst])
                else:
                    P.op(en, lambda e, s=s, kc=kc: e.tensor_copy(out=dst[:, kc, dcol:dcol + ncols], in_=s[:, 0:ncols]),
                         R=[s], W=[dst])
                cast_i[0] += 1

        def project(hT, W, c0, ncols, pout):
            for kc in range(KC):
                P.op("pe", lambda e, kc=kc: e.matmul(pout[:, 0:ncols], lhsT=hT[:, kc, :], rhs=W[:, kc, c0:c0 + ncols],
                                                     start=(kc == 0), stop=(kc == KC - 1)), R=[hT, W], W=[pout])

        def qk_epilogue(pp, gtab, tmpr, ss, sq, dst, nh=4, half=64):
            width = 2 * half
            P.op("act", lambda e: e.activation(out=sq[:, 0:nh * width], in_=pp[:, 0:nh * width], func=AF.Square),
                 R=[pp], W=[sq])
            P.op("dve", lambda e: e.tensor_reduce(out=ss[:, 0:nh], in_=sq[:, 0:nh * width].rearrange("p (h d) -> p h d", h=nh),
                                                  axis=AX.X, op=ALU.add), R=[sq], W=[ss])
            rstd_of(ss, nh, width)
            src = pp.view(pp[:, 0:nh * width].rearrange("p (h d) -> p h d", h=nh))
            rv = sq.view(sq[:, 0:nh * width].rearrange("p (h d) -> p h d", h=nh))
            rope_apply("dve", src, rv, nh, half, gtab, tmpr)
            P.op("dve", lambda e: e.tensor_tensor(out=dst[:, :, :], in0=rv[:, :, :],
                                                  in1=ss[:, 0:nh].unsqueeze(2).broadcast_to([128, nh, width]),
                                                  op=ALU.mult), R=[sq, ss], W=[dst])

        sA = ExitStack()
        if STOP >= 1:
            WA = sb([128, KC, 2112], BF16, "WA", sA)
            stg = [sb([128, 1024], F32, "stg", sA) for _ in range(3)]
            ci = [0]
            load_w_cols(WA, 0, O_AK, 1024, stg, ci)
            load_w_cols(WA, 1024, O_AV, 1024, stg, ci)
            load_w_cols(WA, 2048, O_IK, 64, stg, ci)
            xt = [sb([128, D], F32, "xt", sA) for _ in range(2)]
            junk = sb([128, D], BF16, "junk", sA)
            xns = [sb([128, D], BF16, "xn", sA) for _ in range(2)]
            ssx = [sb([128, 2], F32, "ssx", sA) for _ in range(2)]
            hTs = [sb([128, KC, 128], BF16, "hT", sA) for _ in range(2)]
            tp = [ps([128, 8, 128], BF16, "tp", sA) for _ in range(2)]
            pj = [ps([128, 512], F32, "pj", sA) for _ in range(3)]
            ktp = ps([128, 8, 128], BF16, "ktp", sA)
            kip = ps([128, 1024], BF16, "kip", sA)
            cs = sb([128, 2, 64], F32, "cs", sA)
            rtmp = sb([128, 3, 64], F32, "rtmp", sA)
            rni = sb([128, 64], I32, "rni", sA)
            ktab = sb([128, 4, 64], F32, "ktab", sA)
            itab = sb([128, 4, 32], F32, "itab", sA)
            tmpr = sb([128, 2, 256], F32, "tmpr", sA)
            ssk = sb([128, 8], F32, "ssk", sA)
            sq = sb([128, 512], F32, "sq", sA)
            kn = sb([128, 8, 128], BF16, "kn", sA)
            kin = sb([128, 1, 64], BF16, "kin", sA)
            KTs = [sb([128, 8, 512], BF16, "KTs", sA) for _ in range(2)]
            KIs = [sb([64, 512], BF16, "KIs", sA) for _ in range(2)]
            Vs = [sb([128, 8, 128], BF16, "Vs", sA) for _ in range(2)]
            pji = 0
            def prep(T_):
                x_ = xt[T_ % 2]
                xn = xns[T_ % 2]
                s_ = ssx[T_ % 2]
                h_ = hTs[T_ % 2]
                P.dma("sp", x_[:], xa[T_ * 128:(T_ + 1) * 128, :], x_, W=[x_])
                P.op("act", lambda e, x_=x_, s_=s_: e.activation(out=junk[:], in_=x_[:], func=AF.Square,
                                                              accum_out=s_[:, 0:1]), R=[x_], W=[junk, s_])
                P.op("act", lambda e, s_=s_: e.activation(out=s_[:, 0:1], in_=s_[:, 0:1], func=AF.Sqrt, bias=epsc[:, 0:1],
                                                       scale=1.0 / D), R=[s_, epsc], W=[s_])
                P.op("dve", lambda e, s_=s_: e.reciprocal(out=s_[:, 0:1], in_=s_[:, 0:1]), R=[s_], W=[s_])
                P.op("dve", lambda e, x_=x_, s_=s_: e.tensor_scalar(out=xn[:], in0=x_[:], scalar1=s_[:, 0:1], scalar2=None,
                                                                 op0=ALU.mult), R=[x_, s_], W=[xn])
                for hf in range(2):
                    t_ = tp[hf]
                    for k8 in range(8):
                        kc = hf * 8 + k8
                        P.op("pe", lambda e, t_=t_, k8=k8, kc=kc: e.transpose(t_[:, k8, :], xn[:, kc * 128:(kc + 1) * 128],
                                                                         identb[:]), R=[xn, identb], W=[t_])
                    for k8 in range(8):
                        kc = hf * 8 + k8
                        if k8 % 2 == 0:
                            P.op("act", lambda e, t_=t_, k8=k8, kc=kc, h_=h_: e.activation(
                                out=h_[:, kc, :], in_=t_[:, k8, :], func=AF.Identity, bias=Bmod[:, kc:kc + 1],
                                scale=Amod[:, kc:kc + 1]), R=[t_, Amod, Bmod], W=[h_])
                        else:
                            P.op("dve", lambda e, t_=t_, k8=k8, kc=kc, h_=h_: e.tensor_scalar(
                                out=h_[:, kc, :], in0=t_[:, k8, :], scalar1=Amod[:, kc:kc + 1], scalar2=Bmod[:, kc:kc + 1],
                                op0=ALU.mult, op1=ALU.add), R=[t_, Amod, Bmod], W=[h_])
                P.dma("pool", hT_d[T_], h_[:].rearrange("p k t -> p (k t)"), h_, R=[h_], W=[dr["hT"]])
            prep(0)
            for T_ in range(NTL):
                h_ = hTs[T_ % 2]
                if T_ + 1 < NTL:
                    prep(T_ + 1)
                rope_tables(T_, cs, rtmp, rni)
                gain_tables(cs, gk, ktab, 64, False)
                gain_tables(cs, gi, itab, 32, True)
                slot = (T_ // 4) % 2
                KT_ = KTs[slot]
                KI_ = KIs[slot]
                t4 = T_ % 4
                for gidx in range(2):
                    pp = pj[pji % 3]; pji += 1
                    project(h_, WA, gidx * 512, 512, pp)
                    knv = kn.view(kn[:, gidx * 4:(gidx + 1) * 4, :])
                    qk_epilogue(pp, ktab, tmpr, ssk, sq, knv)
                for hd in range(8):
                    P.op("pe", lambda e, hd=hd: e.transpose(ktp[:, hd, :], kn[:, hd, :], identb[:]), R=[kn, identb], W=[ktp])
                P.op("act", lambda e, KT_=KT_, t4=t4: e.copy(out=KT_[:, :, t4 * 128:(t4 + 1) * 128], in_=ktp[:, :, :]),
                     R=[ktp], W=[KT_])
                V_ = Vs[T_ % 2]
                for gidx in range(2):
                    pp = pj[pji % 3]; pji += 1
                    project(h_, WA, 1024 + gidx * 512, 512, pp)
                    P.op("act", lambda e, pp=pp, gidx=gidx, V_=V_: e.copy(
                        out=V_[:, gidx * 4:(gidx + 1) * 4, :], in_=pp[:, :].rearrange("p (h d) -> p h d", h=4)),
                        R=[pp], W=[V_])
                ch, blk = T_ // 16, T_ % 16
                P.dma("pool", V_d[:, ch, :, blk * 128:(blk + 1) * 128].rearrange("h p d -> p h d"), V_[:], V_,
                      R=[V_], W=[dr["V"]])
                pp = pj[pji % 3]; pji += 1
                project(h_, WA, 2048, 64, pp)
                kiv = kin.view(kin[:, 0:1, :])
                qk_epilogue(pp, itab, tmpr, ssk, sq, kiv, nh=1, half=32)
                P.op("pe", lambda e: e.transpose(kip[0:64, 0:128], kin[:, 0, :], identb[:]), R=[kin, identb], W=[kip])
                P.op("act", lambda e, KI_=KI_, t4=t4: e.copy(out=KI_[:, t4 * 128:(t4 + 1) * 128], in_=kip[0:64, 0:128]),
                     R=[kip], W=[KI_])
                if t4 == 3 or T_ == NTL - 1:
                    nt = t4 + 1
                    t0 = (T_ // 4) * 4 * 128
                    P.dma("pool", KT_d[:, :, t0:t0 + nt * 128].rearrange("h d t -> d h t"), KT_[:, :, 0:nt * 128], KT_,
                          R=[KT_], W=[dr["KT"]])
                    P.dma("pool", KI_d[:, t0:t0 + nt * 128], KI_[:, 0:nt * 128], KI_, R=[KI_], W=[dr["KI"]])

        flush(); sA.close()
        sLB = ExitStack()
        lb = sb([128, 1024], F32, "lb", sLB)
        oml = sb([128, 1024], F32, "oml", sLB)
        sL = ExitStack()
        if STOP >= 2:
            l0 = sb([128, 1024], F32, "l0", sL)
            l1 = sb([128, 1024], F32, "l1", sL)
            P.dma("sp", l0[:], lbl_in[0], l0, W=[l0])
            P.dma("sp", l1[:], lbl_in[1], l1, W=[l1])
            P.op("dve", lambda e: e.tensor_tensor(out=l0[:], in0=l0[:], in1=l1[:], op=ALU.subtract), R=[l0, l1], W=[l0])
            P.op("act", lambda e: e.activation(out=lb[:], in_=l0[:], func=AF.Sigmoid), R=[l0], W=[lb])
            P.op("dve", lambda e: e.tensor_scalar(out=oml[:], in0=lb[:], scalar1=-1.0, scalar2=1.0, op0=ALU.mult, op1=ALU.add),
                 R=[lb], W=[oml])

        flush(); sL.close()
        sB = ExitStack()
        if STOP >= 2:
            WB = sb([128, KC, 2048], BF16, "WB", sB)
            stg = [sb([128, 1024], F32, "stg", sB) for _ in range(3)]
            ci = [0]
            load_w_cols(WB, 0, O_RF, 1024, stg, ci)
            load_w_cols(WB, 1024, O_RI, 1024, stg, ci)
            hTs = [sb([128, KC, 128], BF16, "hT", sB) for _ in range(3)]
            pj = [ps([128, 512], F32, "pj", sB) for _ in range(4)]
            d2p = [ps([128, 512], F32, "d2p", sB) for _ in range(2)]
            csp = [ps([128, 4, 128], F32, "csp", sB) for _ in range(2)]
            sig = sb([128, 1024], F32, "sig", sB)
            logf = sb([128, 1024], F32, "logf", sB)
            kk = sb([128, 1024], F32, "kk", sB)
            E2 = sb([128, 1024], F32, "E2", sB)
            Kl = sb([128, 1024], BF16, "Kl", sB)
            vv = sb([128, 1024], BF16, "vv", sB)
            dec = sb([128, 8, 2], F32, "dec", sB)
            S = sb([128, 8, 128], F32, "S", sB)
            P.op("dve", lambda e: e.memset(S[:], 0.0), W=[S])
            own_tiles = {16 + 16 * g: g for g in range(G)}
            last_needed = 16 + 16 * (G - 1)
            sigs = [sig, sb([128, 1024], F32, "sig2", sB)]
            vvs = [vv, sb([128, 1024], BF16, "vv2", sB)]

            def stageA(T_):
                h_ = hTs[T_ % 3]
                sg_, v_ = sigs[T_ % 2], vvs[T_ % 2]
                P.dma("sp", h_[:].rearrange("p k t -> p (k t)"), hT_d[T_], h_, R=[dr["hT"]], W=[h_])
                for gidx in range(2):
                    project(h_, WB, gidx * 512, 512, pj[gidx])
                    P.op("act", lambda e, gidx=gidx: e.activation(out=sg_[:, gidx * 512:(gidx + 1) * 512], in_=pj[gidx][:, :],
                                                                  func=AF.Sigmoid), R=[pj[gidx]], W=[sg_])
                for gidx in range(2):
                    project(h_, WB, 1024 + gidx * 512, 512, pj[2 + gidx])
                    P.op("act", lambda e, gidx=gidx: e.activation(
                        out=v_[:, gidx * 512:(gidx + 1) * 512], in_=pj[2 + gidx][:, :], func=AF.Copy,
                        scale=validt[:, T_:T_ + 1]), R=[pj[2 + gidx], validt], W=[v_])

            def stageB(T_):
                sg_, v_ = sigs[T_ % 2], vvs[T_ % 2]
                P.op("dve", lambda e: e.tensor_tensor(out=sg_[:], in0=sg_[:], in1=oml[:], op=ALU.mult), R=[sg_, oml], W=[sg_])
                P.op("dve", lambda e: e.tensor_tensor(out=sg_[:], in0=sg_[:], in1=lb[:], op=ALU.add), R=[sg_, lb], W=[sg_])
                P.op("act", lambda e: e.activation(out=logf[:], in_=sg_[:], func=AF.Ln), R=[sg_], W=[logf])
                P.op("dve", lambda e: e.tensor_scalar(out=kk[:], in0=sg_[:], scalar1=-1.0, scalar2=1.0, op0=ALU.mult, op1=ALU.add),
                     R=[sg_], W=[kk])
                for gidx in range(2):
                    P.op("pe", lambda e, gidx=gidx: e.matmul(d2p[gidx][:, :], lhsT=md2[:], rhs=logf[:, gidx * 512:(gidx + 1) * 512],
                                                             start=True, stop=True), R=[md2, logf], W=[d2p[gidx]])
                    P.op("act", lambda e, gidx=gidx: e.activation(out=E2[:, gidx * 512:(gidx + 1) * 512], in_=d2p[gidx][:, :],
                                                                  func=AF.Exp), R=[d2p[gidx]], W=[E2])
                P.op("dve", lambda e: e.tensor_tensor(out=Kl[:], in0=kk[:], in1=E2[:], op=ALU.mult), R=[kk, E2], W=[Kl])
                dp = d2p[0]
                for hd in range(8):
                    P.op("pe", lambda e, hd=hd: e.matmul(dp[:, hd * 2:hd * 2 + 2], lhsT=logf[:, hd * 128:(hd + 1) * 128],
                                                         rhs=cind[:, :], start=True, stop=True), R=[logf, cind], W=[dp])
                P.op("act", lambda e: e.activation(out=dec[:].rearrange("p h c -> p (h c)"), in_=dp[:, 0:16], func=AF.Exp),
                     R=[dp], W=[dec])
                for cj in range(2):
                    for hg_ in range(2):
                        cp = csp[hg_]
                        for h4 in range(4):
                            hd = hg_ * 4 + h4
                            P.op("pe", lambda e, cp=cp, h4=h4, hd=hd, cj=cj: e.matmul(
                                cp[:, h4, :], lhsT=Kl[cj * 64:(cj + 1) * 64, hd * 128:(hd + 1) * 128],
                                rhs=v_[cj * 64:(cj + 1) * 64, hd * 128:(hd + 1) * 128], start=True, stop=True),
                                R=[Kl, v_], W=[cp])
                        for h4 in range(4):
                            hd = hg_ * 4 + h4
                            P.op("dve", lambda e, cp=cp, h4=h4, hd=hd, cj=cj: e.scalar_tensor_tensor(
                                out=S[:, hd, :], in0=S[:, hd, :], scalar=dec[:, hd, cj:cj + 1], in1=cp[:, h4, :],
                                op0=ALU.mult, op1=ALU.add), R=[S, dec, cp], W=[S])

            if last_needed > 0:
                stageA(0)
            for T_ in range(last_needed + 1):
                if T_ in own_tiles:
                    g = own_tiles[T_]
                    P.dma("pool", SN_d[g], S[:].rearrange("p h e -> p (h e)"), S, R=[S], W=[dr["SN"]])
                if T_ == last_needed:
                    break
                if T_ + 1 < last_needed:
                    stageA(T_ + 1)
                stageB(T_)
        flush(); sB.close()
        sC = ExitStack()
        if STOP >= 3:
            hTo = sb([128, KC, NOWN], BF16, "hTo", sC)
            for j in range(NO):
                T_ = 16 + 16 * (j // 2) + (j % 2)
                P.dma("sp", hTo[:, :, j * 128:(j + 1) * 128], hT_d[T_].rearrange("p (k t) -> p k t", k=KC), hTo,
                      R=[dr["hT"]], W=[hTo])
            stg = [sb([128, 512], F32, "stg", sC) for _ in range(4)]
            Wb = [sb([128, KC, 512], BF16, "Wb", sC) for _ in range(2)]
            pj = [ps([128, 512], F32, "pj", sC) for _ in range(2)]
            tpb = [ps([128, 8, 128], BF16, "tpb", sC) for _ in range(2)]
            dps = ps([128, 512], F32, "dps", sC)
            atp = ps([128, 512], F32, "atp", sC)
            op_ = ps([128, 512], F32, "op", sC)
            csp = ps([128, 4, 128], F32, "csp", sC)
            cso = sb([128, NO, 2, 64], F32, "cso", sC)
            rtmp = sb([128, 3, 64], F32, "rtmp", sC)
            rni = sb([128, 64], I32, "rni", sC)
            qtab = sb([128, 4, 64], F32, "qtab", sC)
            itab = sb([128, 4, 32], F32, "itab", sC)
            ones64 = sb([128, 64], F32, "ones64", sC)
            P.op("dve", lambda e: e.memset(ones64[:], 1.0), W=[ones64])
            tmpr = sb([128, 2, 256], F32, "tmpr", sC)
            ssk = sb([128, 8], F32, "ssk", sC)
            sq = sb([128, 512], F32, "sq", sC)
            qn = sb([128, 4, 128], BF16, "qn", sC)
            stq = [sb([128, 4, 128], BF16, "stq", sC) for _ in range(2)]
            agt = [sb([128, 512], BF16, "agt", sC) for _ in range(2)]
            csv = [cso.view(cso[:, j, :, :]) for j in range(NO)]
            for j in range(NO):
                T_ = 16 + 16 * (j // 2) + (j % 2)
                rope_tables(T_, csv[j], rtmp, rni)
            ci = [0]

            def load_block(wb, cols):
                off = 0
                for (c0, n) in cols:
                    for kc in range(KC):
                        s = stg[ci[0] % 4]
                        P.dma("sp", s[:, 0:n], w_in[kc * 128:(kc + 1) * 128, c0:c0 + n], s, W=[s])
                        en = ("pool", "pool", "act")[ci[0] % 3]
                        if en == "act":
                            P.op("act", lambda e, s=s, kc=kc, off=off, n=n: e.copy(out=wb[:, kc, off:off + n], in_=s[:, 0:n]),
                                 R=[s], W=[wb])
                        else:
                            P.op(en, lambda e, s=s, kc=kc, off=off, n=n: e.tensor_copy(out=wb[:, kc, off:off + n], in_=s[:, 0:n]),
                                 R=[s], W=[wb])
                        ci[0] += 1
                    off += n

            bi = 0
            if STOP >= 3.2:
                for qb in range(2):
                    wb = Wb[bi % 2]; bi += 1
                    load_block(wb, [(O_AQ + qb * 512, 512)])
                    for j in range(NO):
                        pp = pj[j % 2]
                        for kc in range(KC):
                            P.op("pe", lambda e, kc=kc, j=j, pp=pp, wb=wb: e.matmul(
                                pp[:, :], lhsT=hTo[:, kc, j * 128:(j + 1) * 128], rhs=wb[:, kc, 0:512],
                                start=(kc == 0), stop=(kc == KC - 1)), R=[hTo, wb], W=[pp])
                        gain_tables(csv[j], gq, qtab, 64, False)
                        qk_epilogue(pp, qtab, tmpr, ssk, sq, qn.view(qn[:, :, :]))
                        tq = tpb[j % 2]
                        for h4 in range(4):
                            P.op("pe", lambda e, h4=h4, tq=tq: e.transpose(tq[:, h4, :], qn[:, h4, :], identb[:]),
                                 R=[qn, identb], W=[tq])
                        sq_ = stq[j % 2]
                        P.op("act", lambda e, sq_=sq_, tq=tq: e.copy(out=sq_[:], in_=tq[:, 0:4, :]), R=[tq], W=[sq_])
                        P.dma("pool", QT_d[qb * 4:(qb + 1) * 4, :, j * 128:(j + 1) * 128].rearrange("h d t -> d h t"),
                              sq_[:], sq_, R=[sq_], W=[dr["QT"]])
            if STOP >= 3.3:
                for gb in range(2):
                    wb = Wb[bi % 2]; bi += 1
                    load_block(wb, [(O_AG + gb * 512, 512)])
                    for j in range(NO):
                        pp = pj[j % 2]
                        for kc in range(KC):
                            P.op("pe", lambda e, kc=kc, j=j, pp=pp, wb=wb: e.matmul(
                                pp[:, :], lhsT=hTo[:, kc, j * 128:(j + 1) * 128], rhs=wb[:, kc, 0:512],
                                start=(kc == 0), stop=(kc == KC - 1)), R=[hTo, wb], W=[pp])
                        a_ = agt[j % 2]
                        P.op("act", lambda e, a_=a_, pp=pp: e.activation(out=a_[:], in_=pp[:, :], func=AF.Silu), R=[pp], W=[a_])
                        P.dma("pool", AG_d[j * 128:(j + 1) * 128, gb * 512:(gb + 1) * 512], a_[:], a_, R=[a_], W=[dr["AG"]])
            if STOP >= 3.4:
                for ib in range(2):
                    wb = Wb[bi % 2]; bi += 1
                    load_block(wb, [(O_IQ + ib * 512, 512)])
                    for j in range(NO):
                        pp = pj[j % 2]
                        for kc in range(KC):
                            P.op("pe", lambda e, kc=kc, j=j, pp=pp, wb=wb: e.matmul(
                                pp[:, :], lhsT=hTo[:, kc, j * 128:(j + 1) * 128], rhs=wb[:, kc, 0:512],
                                start=(kc == 0), stop=(kc == KC - 1)), R=[hTo, wb], W=[pp])
                        gain_tables(csv[j], None, itab, 32, True)
                        src = pp.view(pp[:, :].rearrange("p (h d) -> p h d", h=8))
                        qiv = qn.view(qn[:].rearrange("p a (b d) -> p (a b) d", b=2))
                        rope_apply("dve", src, qiv, 8, 32, itab, tmpr)
                        tq = tpb[j % 2]
                        for h4 in range(4):
                            P.op("pe", lambda e, h4=h4, tq=tq: e.transpose(tq[:, h4, :], qn[:, h4, :], identb[:]),
                                 R=[qn, identb], W=[tq])
                        sq_ = stq[j % 2]
                        P.op("act", lambda e, sq_=sq_, tq=tq: e.copy(out=sq_[:], in_=tq[:, 0:4, :]), R=[tq], W=[sq_])
                        P.dma("pool", QI_d[ib * 4:(ib + 1) * 4, :, j * 128:(j + 1) * 128].rearrange("h d t -> d h t"),
                              sq_[:], sq_, R=[sq_], W=[dr["QI"]])
            if STOP >= 3.5:
                wb = Wb[bi % 2]; bi += 1
                load_block(wb, [(O_IW, 16)])
                for j in range(NO):
                    pp = pj[j % 2]
                    for kc in range(KC):
                        P.op("pe", lambda e, kc=kc, j=j, pp=pp, wb=wb: e.matmul(
                            pp[:, 0:16], lhsT=hTo[:, kc, j * 128:(j + 1) * 128], rhs=wb[:, kc, 0:16],
                            start=(kc == 0), stop=(kc == KC - 1)), R=[hTo, wb], W=[pp])
                    P.op("dve", lambda e, j=j, pp=pp: e.tensor_scalar(out=wsc[:, j, :], in0=pp[:, 0:16], scalar1=0.25 * 0.125,
                                                                   scalar2=None, op0=ALU.mult), R=[pp], W=[wsc])
            Srun = sb([128, 8, 128], F32, "Srun", sC)
            Sb = [sb([128, 128], BF16, "Sb", sC) for _ in range(2)]
            S1 = sb([128, 128], F32, "S1", sC)
            qs = sb([128, 128], F32, "qs", sC)
            gs = sb([128, 128], F32, "gs", sC)
            sg2 = sb([128, 128], F32, "sg2", sC)
            lf = sb([128, 128], F32, "lf", sC)
            k2 = sb([128, 128], F32, "k2", sC)
            v2 = sb([128, 128], BF16, "v2", sC)
            Ex = sb([128, 4, 128], F32, "Ex", sC)
            Qd = sb([128, 128], BF16, "Qd", sC)
            Kd = sb([128, 128], BF16, "Kd", sC)
            Kl2 = sb([128, 128], F32, "Kl2", sC)
            Kz = sb([128, 2, 128], BF16, "Kz", sC)
            Qb = sb([128, 128], BF16, "Qb", sC)
            TT = sb([128, 4, 128], BF16, "TT", sC)
            P.op("dve", lambda e: e.memset(TT[:], 0.0), W=[TT])
            ATs = sb([128, 128], BF16, "ATs", sC)
            dec2 = sb([128, 2], F32, "dec2", sC)
            sso = sb([128, 2], F32, "sso", sC)
            orr = sb([128, 128], BF16, "orr", sC)
            orT = [sb([128, 128], BF16, "orT", sC) for _ in range(2)]
            for hd in range(8 if STOP >= 3.605 else 0):
                wb = Wb[bi % 2]; bi += 1
                load_block(wb, [(O_RQ + hd * 128, 128), (O_RF + hd * 128, 128), (O_RI + hd * 128, 128),
                                (O_RG + hd * 128, 128)])
                for j in range(NO):
                    g = j // 2
                    if j % 2 == 0:
                        P.dma("sp", Srun[:, hd, :], SN_d[g, :, hd * 128:(hd + 1) * 128], Srun, R=[dr["SN"]], W=[Srun])
                    pp = pj[j % 2]
                    for kc in range(KC):
                        P.op("pe", lambda e, kc=kc, j=j, pp=pp, wb=wb: e.matmul(
                            pp[:, :], lhsT=hTo[:, kc, j * 128:(j + 1) * 128], rhs=wb[:, kc, 0:512],
                            start=(kc == 0), stop=(kc == KC - 1)), R=[hTo, wb], W=[pp])
                    P.op("act", lambda e, pp=pp: e.activation(out=qs[:], in_=pp[:, 0:128], func=AF.Silu), R=[pp], W=[qs])
                    P.op("act", lambda e, pp=pp: e.activation(out=sg2[:], in_=pp[:, 128:256], func=AF.Sigmoid), R=[pp], W=[sg2])
                    P.op("act", lambda e, pp=pp: e.copy(out=v2[:], in_=pp[:, 256:384]), R=[pp], W=[v2])
                    P.op("act", lambda e, pp=pp: e.activation(out=gs[:], in_=pp[:, 384:512], func=AF.Silu), R=[pp], W=[gs])
                    P.op("dve", lambda e, hd=hd: e.tensor_tensor(out=sg2[:], in0=sg2[:], in1=oml[:, hd * 128:(hd + 1) * 128],
                                                                 op=ALU.mult), R=[sg2, oml], W=[sg2])
                    P.op("dve", lambda e, hd=hd: e.tensor_tensor(out=sg2[:], in0=sg2[:], in1=lb[:, hd * 128:(hd + 1) * 128],
                                                                 op=ALU.add), R=[sg2, lb], W=[sg2])
                    P.op("act", lambda e: e.activation(out=lf[:], in_=sg2[:], func=AF.Ln), R=[sg2], W=[lf])
                    P.op("dve", lambda e: e.tensor_scalar(out=k2[:], in0=sg2[:], scalar1=-1.0, scalar2=1.0, op0=ALU.mult,
                                                          op1=ALU.add), R=[sg2], W=[k2])
                    if STOP < 3.62:
                        continue
                    for i, m_ in enumerate((md1, md2, md3)):
                        P.op("pe", lambda e, i=i, m_=m_: e.matmul(dps[:, i * 128:(i + 1) * 128], lhsT=m_[:], rhs=lf[:],
                                                                  start=True, stop=True), R=[m_, lf], W=[dps])
                    P.op("pe", lambda e: e.matmul(dps[:, 384:386], lhsT=lf[:], rhs=cind[:, :], start=True, stop=True),
                         R=[lf, cind], W=[dps])
                    P.op("act", lambda e: e.activation(out=Ex[:, 0, :], in_=dps[:, 0:128], func=AF.Exp), R=[dps], W=[Ex])
                    P.op("act", lambda e: e.activation(out=Ex[:, 1, :], in_=dps[:, 0:128], func=AF.Exp, scale=-1.0),
                         R=[dps], W=[Ex])
                    P.op("act", lambda e: e.activation(out=Ex[:, 2, :], in_=dps[:, 128:256], func=AF.Exp), R=[dps], W=[Ex])
                    P.op("act", lambda e: e.activation(out=Ex[:, 3, :], in_=dps[:, 256:384], func=AF.Exp), R=[dps], W=[Ex])
                    P.op("act", lambda e: e.activation(out=dec2[:], in_=dps[:, 384:386], func=AF.Exp), R=[dps], W=[dec2])
                    P.op("dve", lambda e: e.tensor_tensor(out=Qd[:], in0=qs[:], in1=Ex[:, 0, :], op=ALU.mult), R=[qs, Ex], W=[Qd])
                    P.op("dve", lambda e: e.tensor_tensor(out=Kd[:], in0=k2[:], in1=Ex[:, 1, :], op=ALU.mult), R=[k2, Ex], W=[Kd])
                    P.op("dve", lambda e: e.tensor_tensor(out=Kl2[:], in0=k2[:], in1=Ex[:, 2, :], op=ALU.mult), R=[k2, Ex], W=[Kl2])
                    P.op("dve", lambda e: e.tensor_tensor(out=Qb[:], in0=qs[:], in1=Ex[:, 3, :], op=ALU.mult), R=[qs, Ex], W=[Qb])
                    if STOP < 3.63:
                        continue
                    tq = tpb[j % 2]
                    for i, s_ in enumerate((Qd, Kd, Qb)):
                        P.op("pe", lambda e, i=i, s_=s_, tq=tq: e.transpose(tq[:, i, :], s_[:], identb[:]),
                             R=[s_, identb], W=[tq])
                    P.op("act", lambda e, tq=tq: e.copy(out=TT[:, 0:2, :], in_=tq[:, 0:2, :]), R=[tq], W=[TT])
                    P.op("act", lambda e, tq=tq: e.copy(out=TT[:, 2, 0:64], in_=tq[:, 2, 0:64]), R=[tq], W=[TT])
                    P.op("act", lambda e, tq=tq: e.copy(out=TT[:, 3, 64:128], in_=tq[:, 2, 64:128]), R=[tq], W=[TT])
                    P.op("pe", lambda e: e.matmul(atp[:, 0:128], lhsT=TT[:, 1, :], rhs=TT[:, 0, :], start=True, stop=True),
                         R=[TT], W=[atp])
                    P.op("dve", lambda e: e.tensor_tensor(out=ATs[:], in0=atp[:, 0:128], in1=cmask[:], op=ALU.mult),
                         R=[atp, cmask], W=[ATs])
                    if STOP < 3.64:
                        continue
                    P.op("dve", lambda e, hd=hd: e.tensor_copy(out=Sb[0][:], in_=Srun[:, hd, :]), R=[Srun], W=[Sb[0]])
                    for cj in range(2):
                        P.op("dve", lambda e, cj=cj: e.tensor_scalar(out=Kz[:, cj, :], in0=Kl2[:], scalar1=cind[:, cj:cj + 1],
                                                                     scalar2=None, op0=ALU.mult), R=[Kl2, cind], W=[Kz])
                    for cj in range(2):
                        P.op("pe", lambda e, cj=cj: e.matmul(csp[:, cj, :], lhsT=Kz[:, cj, :], rhs=v2[:], start=True, stop=True),
                             R=[Kz, v2], W=[csp])
                    P.op("dve", lambda e, hd=hd: e.scalar_tensor_tensor(out=S1[:], in0=Srun[:, hd, :], scalar=dec2[:, 0:1],
                                                                        in1=csp[:, 0, :], op0=ALU.mult, op1=ALU.add),
                         R=[Srun, dec2, csp], W=[S1])
                    P.op("dve", lambda e: e.tensor_copy(out=Sb[1][:], in_=S1[:]), R=[S1], W=[Sb[1]])
                    P.op("dve", lambda e, hd=hd: e.scalar_tensor_tensor(out=Srun[:, hd, :], in0=S1[:], scalar=dec2[:, 1:2],
                                                                        in1=csp[:, 1, :], op0=ALU.mult, op1=ALU.add),
                         R=[S1, dec2, csp], W=[Srun])
                    if STOP < 3.65:
                        continue
                    P.op("pe", lambda e: e.matmul(op_[:, 0:128], lhsT=ATs[:], rhs=v2[:], start=True, stop=False),
                         R=[ATs, v2], W=[op_])
                    P.op("pe", lambda e: e.matmul(op_[:, 0:128], lhsT=TT[:, 2, :], rhs=Sb[0][:], start=False, stop=False),
                         R=[TT, Sb[0]], W=[op_])
                    P.op("pe", lambda e: e.matmul(op_[:, 0:128], lhsT=TT[:, 3, :], rhs=Sb[1][:], start=False, stop=True),
                         R=[TT, Sb[1]], W=[op_])
                    P.op("act", lambda e: e.activation(out=sq[:, 0:128], in_=op_[:, 0:128], func=AF.Square, accum_out=sso[:, 0:1]),
                         R=[op_], W=[sq, sso])
                    rstd_of(sso, 1, 128.0)
                    P.op("dve", lambda e: e.tensor_tensor(out=gs[:], in0=gs[:], in1=hg[:], op=ALU.mult), R=[gs, hg], W=[gs])
                    P.op("dve", lambda e: e.scalar_tensor_tensor(out=orr[:], in0=op_[:, 0:128], scalar=sso[:, 0:1], in1=gs[:],
                                                                 op0=ALU.mult, op1=ALU.mult), R=[op_, sso, gs], W=[orr])
                    if STOP < 3.66:
                        continue
                    P.op("pe", lambda e, tq=tq: e.transpose(tq[:, 3, :], orr[:], identb[:]), R=[orr, identb], W=[tq])
                    o_ = orT[j % 2]
                    P.op("act", lambda e, o_=o_, tq=tq: e.copy(out=o_[:], in_=tq[:, 3, :]), R=[tq], W=[o_])
                    P.dma("pool", CT_d[8 + hd, :, j * 128:(j + 1) * 128], o_[:], o_, R=[o_], W=[dr["CT"]])

        flush(); sC.close(); sLB.close()
        sD = ExitStack()
        if STOP >= 4:
            kiT = sb([128, NT], BF16, "kiT", sD)
            P.dma("sp", kiT[0:64, :], KI_d[:, :], kiT, R=[dr["KI"]], W=[kiT])
            P.dma("sp", kiT[64:128, :], KI_d[:, :], kiT, R=[dr["KI"]], W=[kiT])
            vb = sb([128, 2048], F32, "vb", sD)
            P.dma("sp", vb[:], vbias_in[:, :], vb, W=[vb])
            cb = sb([128, 2, 256], F32, "cb", sD)
            P.dma("sp", cb[:], cbias_in[:, :, :], cb, W=[cb])
            score = sb([128, NT], F32, "score", sD)
            maskT = sb([128, NB, 256], U8, "maskT", sD)
            JW = 4096
            junk = sb([128, JW], U8, "junkc", sD)
            QTs = sb([128, 8, 256], BF16, "QTs", sD)
            QIs = sb([128, 8, 256], BF16, "QIs", sD)
            AGs = sb([128, 2, 1024], BF16, "AGs", sD)
            diag = sb([128, 16, 128], BF16, "diag", sD)
            Rr = [sb([128, 512], BF16, "R", sD) for _ in range(4)]
            mk = [sb([128, 512], BF16, "mk", sD) for _ in range(2)]
            stat = sb([128, 2, 40], F32, "stat", sD)
            bis = sb([128, 8], F32, "bis", sD)
            cparts = sb([128, 8], F32, "cparts", sD)
            Kr = [sb([128, 2048], BF16, "Kr", sD) for _ in range(2)]
            Vr = [sb([128, 16, 129], BF16, "Vr", sD) for _ in range(2)]
            for v_ in Vr:
                P.op("dve", lambda e, v_=v_: e.memset(v_[:, :, 128:129], 1.0), W=[v_])
            Pe = [sb([128, 256], BF16, "Pe", sD) for _ in range(3)]
            Pm = [sb([128, 256], BF16, "Pm", sD) for _ in range(3)]
            rec = sb([128, 2], F32, "rec", sD)
            oa = sb([128, 128], BF16, "oa", sD)
            oaT = [sb([128, 2, 128], BF16, "oaT", sD) for _ in range(2)]
            lgp = [ps([128, 512], F32, "lgp", sD) for _ in range(2)]
            scp = [ps([128, 512], F32, "scp", sD) for _ in range(1)]
            mtp = ps([128, 8, 128], BF16, "mtp", sD)
            stp = [ps([128, 512], F32, "stp", sD) for _ in range(2)]
            acc = [ps([128, 512], F32, "acc", sD) for _ in range(2)]
            lgp4 = [lgp[0], lgp[1], stp[0], stp[1]]
            ri = 0
            for g in range(min(G, KSEG)):
                N = 2048 * (g + 1) + 256
                nkt = (N + 511) // 512
                tok0 = g * 256
                P.dma("sp", QTs[:], QT_d[:, :, tok0:tok0 + 256].rearrange("h d t -> d h t"), QTs, R=[dr["QT"]], W=[QTs])
                P.dma("sp", QIs[:], QI_d[:, :, tok0:tok0 + 256].rearrange("h d t -> d h t"), QIs, R=[dr["QI"]], W=[QIs])
                P.dma("sp", AGs[:], AG_d[tok0:tok0 + 256, :].rearrange("(q p) n -> p q n", p=128), AGs, R=[dr["AG"]], W=[AGs])
                for qt in range(2 if KIDX else 0):
                    j = 2 * g + qt
                    for h in range(16):
                        P.op("dve", lambda e, h=h, j=j: e.tensor_scalar(out=diag[:, h, :], in0=identf[:], scalar1=wsc[:, j, h:h + 1],
                                                                        scalar2=None, op0=ALU.mult), R=[identf, wsc], W=[diag])
                    for kt in range(nkt):
                        k0 = kt * 512
                        kw = min(512, N - k0)
                        sp_ = scp[0]
                        def logits(h):
                            hp, hh = h // 2, h % 2
                            lg = lgp4[h % 4]
                            P.op("pe", lambda e, lg=lg, hp=hp, hh=hh, qt=qt, k0=k0, kw=kw: e.matmul(
                                lg[:, 0:kw], lhsT=QIs[hh * 64:(hh + 1) * 64, hp, qt * 128:(qt + 1) * 128],
                                rhs=kiT[hh * 64:(hh + 1) * 64, k0:k0 + kw], start=True, stop=True), R=[QIs, kiT], W=[lg])
                        logits(0)
                        logits(1)
                        for h in range(16):
                            lg = lgp4[h % 4]
                            if h + 2 < 16:
                                logits(h + 2)
                            r_ = Rr[ri % 4]; ri += 1
                            if h % 2 == 0:
                                P.op("act", lambda e, r_=r_, lg=lg, kw=kw: e.activation(out=r_[:, 0:kw], in_=lg[:, 0:kw], func=AF.Relu),
                                     R=[lg], W=[r_])
                            else:
                                P.op("dve", lambda e, r_=r_, lg=lg, kw=kw: e.tensor_scalar(out=r_[:, 0:kw], in0=lg[:, 0:kw], scalar1=0.0,
                                                                                      scalar2=None, op0=ALU.max), R=[lg], W=[r_])
                            P.op("pe", lambda e, sp_=sp_, h=h, r_=r_, kw=kw: e.matmul(
                                sp_[:, 0:kw], lhsT=diag[:, h, :], rhs=r_[:, 0:kw], start=(h == 0), stop=(h == 15)),
                                R=[diag, r_], W=[sp_])
                        P.op("dve", lambda e, sp_=sp_, kt=kt, kw=kw: e.tensor_reduce(out=stat[:, 0, kt:kt + 1], in_=sp_[:, 0:kw],
                                                                                 axis=AX.X, op=ALU.min), R=[sp_], W=[stat])
                        if kt < 4:
                            P.op("dve", lambda e, sp_=sp_, k0=k0, kw=kw: e.tensor_tensor(out=score[:, k0:k0 + kw], in0=sp_[:, 0:kw],
                                                                                     in1=vb[:, k0:k0 + kw], op=ALU.add),
                                 R=[sp_, vb], W=[score])
                        elif kt == nkt - 1:
                            P.op("dve", lambda e, sp_=sp_, k0=k0, kw=kw, qt=qt: e.tensor_tensor(
                                out=score[:, k0:k0 + kw], in0=sp_[:, 0:kw], in1=cb[:, qt, :], op=ALU.add), R=[sp_, cb], W=[score])
                        else:
                            P.op("dve", lambda e, sp_=sp_, k0=k0, kw=kw: e.tensor_copy(out=score[:, k0:k0 + kw], in_=sp_[:, 0:kw]),
                                 R=[sp_], W=[score])
                    P.op("dve", lambda e: e.tensor_reduce(out=bis[:, 1:2], in_=score[:, 0:N], axis=AX.X, op=ALU.max),
                         R=[score], W=[bis])
                    P.op("dve", lambda e, nkt=nkt: e.tensor_reduce(out=bis[:, 0:1], in_=stat[:, 0, 0:nkt], axis=AX.X, op=ALU.min),
                         R=[stat], W=[bis])
                    P.op("dve", lambda e: e.tensor_scalar(out=bis[:, 0:1], in0=bis[:, 0:1], scalar1=-1.0, scalar2=None, op0=ALU.add),
                         R=[bis], W=[bis])
                    P.op("dve", lambda e: e.tensor_scalar(out=bis[:, 1:2], in0=bis[:, 1:2], scalar1=1.0, scalar2=None, op0=ALU.add),
                         R=[bis], W=[bis])
                    nparts = (N + JW - 1) // JW
                    for it in range(NIT):
                        P.op("dve", lambda e: e.tensor_tensor(out=bis[:, 2:3], in0=bis[:, 0:1], in1=bis[:, 1:2], op=ALU.add),
                             R=[bis], W=[bis])
                        P.op("dve", lambda e: e.tensor_scalar(out=bis[:, 2:3], in0=bis[:, 2:3], scalar1=0.5, scalar2=None, op0=ALU.mult),
                             R=[bis], W=[bis])
                        for pi_ in range(nparts):
                            a0 = pi_ * JW
                            aw = min(JW, N - a0)
                            P.op("dve", lambda e, a0=a0, aw=aw, pi_=pi_: e.tensor_scalar(
                                out=junk[:, 0:aw], in0=score[:, a0:a0 + aw], scalar1=bis[:, 2:3], scalar2=None,
                                op0=ALU.is_ge, op1=ALU.add, accum_out=cparts[:, pi_:pi_ + 1]), R=[score, bis], W=[junk, cparts])
                        P.op("dve", lambda e, nparts=nparts: e.tensor_reduce(out=bis[:, 3:4], in_=cparts[:, 0:nparts], axis=AX.X, op=ALU.add),
                             R=[cparts], W=[bis])
                        P.op("dve", lambda e: e.tensor_scalar(out=bis[:, 4:5], in0=bis[:, 3:4], scalar1=TOPK - 0.5, scalar2=None,
                                                              op0=ALU.is_ge), R=[bis], W=[bis])
                        P.op("dve", lambda e: e.tensor_tensor(out=bis[:, 5:6], in0=bis[:, 2:3], in1=bis[:, 0:1], op=ALU.subtract),
                             R=[bis], W=[bis])
                        P.op("dve", lambda e: e.scalar_tensor_tensor(out=bis[:, 0:1], in0=bis[:, 5:6], scalar=bis[:, 4:5], in1=bis[:, 0:1],
                                                                     op0=ALU.mult, op1=ALU.add), R=[bis], W=[bis])
                        P.op("dve", lambda e: e.tensor_tensor(out=bis[:, 5:6], in0=bis[:, 1:2], in1=bis[:, 2:3], op=ALU.subtract),
                             R=[bis], W=[bis])
                        P.op("dve", lambda e: e.scalar_tensor_tensor(out=bis[:, 1:2], in0=bis[:, 5:6], scalar=bis[:, 4:5], in1=bis[:, 2:3],
                                                                     op0=ALU.mult, op1=ALU.add), R=[bis], W=[bis])
                    for kt in range(nkt):
                        k0 = kt * 512
                        kw = min(512, N - k0)
                        m_ = mk[kt % 2]
                        P.op("dve", lambda e, m_=m_, k0=k0, kw=kw: e.tensor_scalar(out=m_[:, 0:kw], in0=score[:, k0:k0 + kw],
                                                                               scalar1=bis[:, 0:1], scalar2=None, op0=ALU.is_ge),
                             R=[score, bis], W=[m_])
                        nb4 = kw // 128
                        for b4 in range(nb4):
                            P.op("pe", lambda e, m_=m_, b4=b4: e.transpose(mtp[:, b4, :], m_[:, b4 * 128:(b4 + 1) * 128], identb[:]),
                                 R=[m_, identb], W=[mtp])
                        P.op("act", lambda e, kt=kt, nb4=nb4, qt=qt: e.copy(out=maskT[:, kt * 4:kt * 4 + nb4, qt * 128:(qt + 1) * 128],
                                                                        in_=mtp[:, 0:nb4, :]), R=[mtp], W=[maskT])
                nblk = N // 128
                for hd in range(8 if KATT else 0):
                    bidx = 0
                    for ch in range(g + 2):
                        nbc = 16 if ch < g + 1 else 2
                        K_ = Kr[(hd * (g + 2) + ch) % 2]
                        V_ = Vr[(hd * (g + 2) + ch) % 2]
                        P.dma("sp", K_[:, 0:nbc * 128], KT_d[hd, :, ch * 2048:ch * 2048 + nbc * 128], K_, R=[dr["KT"]], W=[K_])
                        P.dma("sp", V_[:, 0:nbc, 0:128], V_d[hd, ch, :, 0:nbc * 128].rearrange("p (b d) -> p b d", d=128), V_,
                              R=[dr["V"]], W=[V_])
                        def smat(b_):
                            kb = ch * 16 + b_
                            st_ = stp[kb % 2]
                            P.op("pe", lambda e, st_=st_, K_=K_, b_=b_, hd=hd: e.matmul(
                                st_[:, 0:256], lhsT=K_[:, b_ * 128:(b_ + 1) * 128], rhs=QTs[:, hd, :], start=True, stop=True),
                                R=[K_, QTs], W=[st_])
                        smat(0)
                        for b_ in range(nbc):
                            kb = ch * 16 + b_
                            st_ = stp[kb % 2]
                            pe_ = Pe[kb % 3]
                            pm_ = Pm[kb % 3]
                            P.op("act", lambda e, pe_=pe_, st_=st_: e.activation(out=pe_[:], in_=st_[:, 0:256], func=AF.Exp,
                                                                               scale=float(128 ** -0.5)), R=[st_], W=[pe_])
                            P.op("dve", lambda e, pm_=pm_, pe_=pe_, kb=kb: e.tensor_tensor(out=pm_[:], in0=pe_[:], in1=maskT[:, kb, :],
                                                                                        op=ALU.mult), R=[pe_, maskT], W=[pm_])
                            if b_ + 1 < nbc:
                                smat(b_ + 1)
                            for qt in range(2):
                                P.op("pe", lambda e, qt=qt, pm_=pm_, V_=V_, b_=b_, kb=kb, nblk=nblk: e.matmul(
                                    acc[qt][:, 0:129], lhsT=pm_[:, qt * 128:(qt + 1) * 128], rhs=V_[:, b_, :],
                                    start=(kb == 0), stop=(kb == nblk - 1)), R=[pm_, V_], W=[acc[qt]])
                    o2 = oaT[hd % 2]
                    for qt in range(2):
                        P.op("dve", lambda e, qt=qt: e.reciprocal(out=rec[:, qt:qt + 1], in_=acc[qt][:, 128:129]), R=[acc[qt]], W=[rec])
                        P.op("dve", lambda e, qt=qt, hd=hd: e.scalar_tensor_tensor(
                            out=oa[:], in0=acc[qt][:, 0:128], scalar=rec[:, qt:qt + 1], in1=AGs[:, qt, hd * 128:(hd + 1) * 128],
                            op0=ALU.mult, op1=ALU.mult), R=[acc[qt], rec, AGs], W=[oa])
                        P.op("pe", lambda e, qt=qt: e.transpose(mtp[:, qt, :], oa[:], identb[:]), R=[oa, identb], W=[mtp])
                    P.op("act", lambda e, o2=o2: e.copy(out=o2[:], in_=mtp[:, 0:2, :]), R=[mtp], W=[o2])
                    P.dma("pool", CT_d[hd, :, tok0:tok0 + 256], o2[:].rearrange("p q t -> p (q t)"), o2, R=[o2], W=[dr["CT"]])

        flush(); sD.close()
        sE = ExitStack()
        if STOP >= 5:
            Wo = sb([128, KC, D], BF16, "Wo", sE)
            stg = [sb([128, 1024], F32, "stg", sE) for _ in range(3)]
            ci = 0
            for kc in range(KC):
                for hf in range(2):
                    s = stg[ci % 3]
                    P.dma("sp", s[:], w_out[kc * 128:(kc + 1) * 128, hf * 1024:(hf + 1) * 1024], s, W=[s])
                    en = ("pool", "dve", "act")[ci % 3]
                    if en == "act":
                        P.op("act", lambda e, s=s, kc=kc, hf=hf: e.copy(out=Wo[:, kc, hf * 1024:(hf + 1) * 1024], in_=s[:]),
                             R=[s], W=[Wo])
                    else:
                        P.op(en, lambda e, s=s, kc=kc, hf=hf: e.tensor_copy(out=Wo[:, kc, hf * 1024:(hf + 1) * 1024], in_=s[:]),
                             R=[s], W=[Wo])
                    ci += 1
            gbc = sb([128, D], F32, "gbc", sE)
            gp = ps([128, 512], F32, "gp", sE)
            for nb in range(4):
                P.op("pe", lambda e, nb=nb: e.matmul(gp[:, :], lhsT=ones1[0:1, :], rhs=grow[0:1, nb * 512:(nb + 1) * 512],
                                                     start=True, stop=True), R=[ones1, grow], W=[gp])
                P.op("act", lambda e, nb=nb: e.copy(out=gbc[:, nb * 512:(nb + 1) * 512], in_=gp[:, :]), R=[gp], W=[gbc])
            CTs = [sb([128, KC, 128], BF16, "CTs", sE) for _ in range(2)]
            xo = [sb([128, D], F32, "xo", sE) for _ in range(2)]
            yo = [sb([128, D], F32, "yo", sE) for _ in range(2)]
            mp = [ps([128, 512], F32, "mp", sE) for _ in range(3)]
            mi = 0
            for j in range(NO):
                T_ = 16 + 16 * (j // 2) + (j % 2)
                c_ = CTs[j % 2]
                x_ = xo[j % 2]
                y_ = yo[j % 2]
                P.dma("sp", c_[:], CT_d[:, :, j * 128:(j + 1) * 128].rearrange("k p t -> p k t"), c_, R=[dr["CT"]], W=[c_])
                P.dma("sp", x_[:], xa[T_ * 128:(T_ + 1) * 128, :], x_, W=[x_])
                for nb in range(4):
                    m_ = mp[mi % 3]; mi += 1
                    for kc in range(KC):
                        P.op("pe", lambda e, m_=m_, c_=c_, kc=kc, nb=nb: e.matmul(
                            m_[:, :], lhsT=c_[:, kc, :], rhs=Wo[:, kc, nb * 512:(nb + 1) * 512],
                            start=(kc == 0), stop=(kc == KC - 1)), R=[c_, Wo], W=[m_])
                    P.op("dve", lambda e, m_=m_, y_=y_, nb=nb: e.tensor_tensor(out=y_[:, nb * 512:(nb + 1) * 512], in0=m_[:, :],
                                                                            in1=gbc[:, nb * 512:(nb + 1) * 512], op=ALU.mult),
                         R=[m_, gbc], W=[y_])
                    P.op("pool", lambda e, x_=x_, y_=y_, nb=nb: e.tensor_tensor(out=y_[:, nb * 512:(nb + 1) * 512],
                                                                             in0=y_[:, nb * 512:(nb + 1) * 512],
                                                                             in1=x_[:, nb * 512:(nb + 1) * 512], op=ALU.add),
                         R=[x_, y_], W=[y_])
                P.dma("pool", y[j * 128:(j + 1) * 128, :], y_[:], y_, R=[y_], W=[dr["y"]])
            P.wait_all("pool", yo)
            P.wait_all("sp", yo)

        flush(); sE.close()
    return nc


_CACHE = {}


def _consts():
    p = np.arange(128)
    ch = p // 64
    same = ch[:, None] == ch[None, :]
    tri = (p[:, None] <= p[None, :]) & same
    mid = (p[:, None] <= (ch[None, :] * 64 + 31)) & same
    last = same
    md1 = tri.astype(np.float32) - mid.astype(np.float32)
    md2 = last.astype(np.float32) - tri.astype(np.float32)
    md3 = tri.astype(np.float32)
    cind = np.stack([(ch == 0), (ch == 1)], axis=1).astype(np.float32)
    cmask = tri.astype(np.float32)
    half = 64
    invf = (10000.0 ** (-np.arange(half, dtype=np.float32) / half)).astype(np.float32)
    cb = np.zeros((128, 2, 256), np.float32)
    for qt in range(2):
        qc = (qt * 128 + p) // 64
        kc = np.arange(256) // 64
        cb[:, qt, :] = np.where(kc[None, :] <= qc[:, None], 0.0, -BIG)
    return dict(md1=md1, md2=md2, md3=md3, cind=cind, cmask=cmask,
                invf=np.tile(invf[None, :], (128, 1)).astype(np.float32),
                identf=np.eye(128, dtype=np.float32), cbias=cb)


def kernel(x, c, positions, ada_w, ada_b, norm_g, w_in, q_norm_g, k_norm_g, idx_k_norm_g,
           hgrn_lb_logits, hgrn_norm_g, w_out):
    x = np.asarray(x, np.float32)[0]
    S = x.shape[0]
    G = S // 2048
    NT = 2048 * G + 256
    NTL = NT // 128
    if G not in _CACHE:
        _CACHE[G] = build(G)
    nc = _CACHE[G]
    cst = _consts()
    pos = np.asarray(positions, np.int32)[0]
    ada_b = np.asarray(ada_b, np.float32)[0]
    common = dict(
        cfm=np.ascontiguousarray(np.asarray(c, np.float32)[0].reshape(KC, 128).T),
        ada_w=np.ascontiguousarray(np.asarray(ada_w, np.float32)[0]),
        adab_fm=np.ascontiguousarray(ada_b.reshape(48, 128).T),
        adab_g=np.ascontiguousarray(ada_b[2 * D:3 * D][None, :]),
        ng_fm=np.ascontiguousarray(np.asarray(norm_g, np.float32)[0].reshape(KC, 128).T),
        w_in=np.ascontiguousarray(np.asarray(w_in, np.float32)[0]),
        w_out=np.ascontiguousarray(np.asarray(w_out, np.float32)[0]),
        gq=np.tile(np.asarray(q_norm_g, np.float32)[0][None, :], (128, 1)),
        gk=np.tile(np.asarray(k_norm_g, np.float32)[0][None, :], (128, 1)),
        gi=np.tile(np.asarray(idx_k_norm_g, np.float32)[0][None, :], (128, 1)),
        hg=np.tile(np.asarray(hgrn_norm_g, np.float32)[0][None, :], (128, 1)),
        lbl=np.ascontiguousarray(np.tile(np.asarray(hgrn_lb_logits, np.float32)[:, None, :], (1, 128, 1))),
        **cst,
    )
    in_maps = []
    for cc in range(8):
        npad = 2048 - 256 * cc
        nreal = NT - npad
        xa = np.zeros((NT, D), np.float32)
        xa[npad:] = x[:nreal]
        pa = np.zeros((NT,), np.int32)
        pa[npad:] = pos[:nreal]
        va = np.zeros((NT,), np.float32)
        va[npad:] = 1.0
        vbias = np.where(va[:2048] > 0, 0.0, -BIG).astype(np.float32)
        m = dict(common)
        m.update(xa=xa, posa=np.ascontiguousarray(pa.reshape(NTL, 128).T),
                 valid=np.ascontiguousarray(va.reshape(NTL, 128).T),
                 vbias=np.tile(vbias[None, :], (128, 1)))
        in_maps.append(m)
    res = run_bass_kernel_spmd(nc, in_maps, core_ids=list(range(8)))
    out = np.zeros((1, S, D), np.float32)
    for cc in range(8):
        yc = res.results[cc]["y"]
        for g in range(G):
            t0 = 2048 * g + 256 * cc
            out[0, t0:t0 + 256] = yc[g * 256:(g + 1) * 256]
    return out
```

```python
import numpy as np
import ml_dtypes
import concourse.bass as bass
import concourse.mybir as mybir
from concourse.bass_utils import run_bass_kernel_spmd

F32 = mybir.dt.float32
BF16 = mybir.dt.bfloat16
I32 = mybir.dt.int32
U8 = mybir.dt.uint8
AF = mybir.ActivationFunctionType
ALU = mybir.AluOpType
AX = mybir.AxisListType

D = 2048
NIN = 9296
KC = 16
EPS = 1e-6
NIT = 12
TOPK = 256
BIG = 1.0e30
PI = float(np.pi)
C1 = 6.28125
C2 = float(2.0 * np.pi - 6.28125)
O_AQ, O_AK, O_AV, O_AG, O_IQ, O_IK, O_IW, O_RQ, O_RF, O_RI, O_RG = (
    0, 1024, 2048, 3072, 4096, 5120, 5184, 5200, 6224, 7248, 8272)


class Buf:
    def __init__(self, name):
        self.name = name
        self.w = {}
        self.r = {}
        self.dsem = None
        self.dcnt = 0
        self.noww = False


class T:
    def __init__(self, t, name, b=None):
        self.t = t
        self.b = b if b is not None else Buf(name)

    def __getitem__(self, k):
        return self.t[k]

    def view(self, ap):
        return T(ap, self.b.name, self.b)


class _Rec:
    def __init__(self):
        self.call = None

    def __getattr__(self, name):
        def f(*a, **kw):
            self.call = (name, a, kw)
            return self
        return f


class Eng:
    def __init__(self, name, sem, is_pe=False, is_q=False):
        self.name = name
        self.sem = sem
        self.cnt = 0
        self.known = {}
        self.q = []
        self.is_pe = is_pe


class Prog:
    def __init__(self, nc):
        self.nc = nc
        self.es = None
        self.stack = None
        self.engs = {}
        self.nsem = 0
        self.dbufs = []

    def new_sem(self, name):
        self.nsem += 1
        return self.stack.enter_context(self.nc.semaphore(name))

    def setup(self):
        for n, pe in (("pe", True), ("act", False), ("dve", False), ("pool", False), ("sp", False)):
            self.engs[n] = Eng(n, self.new_sem("s_" + n), is_pe=pe)

    def _deps(self, e, R, W):
        deps = {}
        for t in list(R) + [w_ for w_ in W if not w_.b.noww]:
            for s, v in t.b.w.items():
                if deps.get(s, 0) < v:
                    deps[s] = v
        for t in W:
            for s, v in t.b.r.items():
                if deps.get(s, 0) < v:
                    deps[s] = v
        for s, v in deps.items():
            if s is e.sem and e.is_pe:
                continue
            if e.known.get(s, 0) < v:
                e.q.append(("w", s, v))
                e.known[s] = v

    def op(self, en, fn, R=(), W=()):
        e = self.engs[en]
        self._deps(e, R, W)
        e.cnt += 1
        rec = _Rec()
        fn(rec)
        e.q.append(("i", rec.call, e.sem, 1))
        ev = (e.sem, e.cnt)
        for t in W:
            t.b.w[ev[0]] = ev[1]
            t.b.r = {}
        for t in R:
            t.b.r[ev[0]] = ev[1]

    def dma(self, en, out, in_, sb, R=(), W=()):
        e = self.engs[en]
        self._deps(e, R, W)
        if sb.b.dsem is None:
            sb.b.dsem = self.new_sem("d_" + sb.b.name)
            self.dbufs.append(sb.b)
        sb.b.dcnt += 16
        sem, val = sb.b.dsem, sb.b.dcnt
        e.q.append(("d", out, in_, sem))
        for t in W:
            t.b.w[sem] = val
            t.b.r = {}
        for t in R:
            t.b.r[sem] = val

    def barrier(self):
        evs = [(e.sem, e.cnt) for e in self.engs.values() if e.cnt > 0]
        evs += [(b.dsem, b.dcnt) for b in self.dbufs]
        for e in self.engs.values():
            for s_, v in evs:
                if s_ is e.sem:
                    continue
                if e.known.get(s_, 0) < v:
                    e.q.append(("w", s_, v))
                    e.known[s_] = v

    def wait_all(self, en, ts):
        e = self.engs[en]
        self._deps(e, ts, ts)

    def emit(self, block):
        nc = self.nc
        m = {"pe": block.tensor, "act": block.scalar, "dve": block.vector, "pool": block.gpsimd,
             "sp": block.sync}
        for n, e in self.engs.items():
            def body(eng, e=e):
                for it in e.q:
                    if it[0] == "w":
                        eng.wait_ge(it[1], it[2])
                    elif it[0] == "i":
                        nm, a_, kw_ = it[1]
                        getattr(eng, nm)(*a_, **kw_).then_inc(it[2], it[3])
                    else:
                        eng.dma_start(out=it[1], in_=it[2]).then_inc(it[3], 16)
            m[n](body)
        for e in self.engs.values():
            e.q = []


def build(G):
    import os
    STOP = float(os.environ.get('KSTOP', '99'))
    KSEG = int(os.environ.get('KSEG', '99'))
    KATT = int(os.environ.get('KATT', '1'))
    KIDX = int(os.environ.get('KIDX', '1'))
    NT = 2048 * G + 256
    NTL = NT // 128
    NOWN = 256 * G
    NO = 2 * G
    NCH = G + 1
    NB = NT // 128

    nc = bass.Bass("TRN2", target_bir_lowering=False)
    dt = nc.dram_tensor

    def ext(name, shape, dtype=F32):
        return dt(name, list(shape), dtype, kind="ExternalInput").ap()

    xa = ext("xa", [NT, D])
    posa = ext("posa", [128, NTL], I32)
    valid = ext("valid", [128, NTL])
    vbias_in = ext("vbias", [128, 2048])
    cfm = ext("cfm", [128, KC])
    ada_w = ext("ada_w", [D, 3 * D])
    adab_fm = ext("adab_fm", [128, 48])
    adab_g = ext("adab_g", [1, D])
    ng_fm = ext("ng_fm", [128, KC])
    w_in = ext("w_in", [D, NIN])
    w_out = ext("w_out", [D, D])
    gq_in = ext("gq", [128, 128])
    gk_in = ext("gk", [128, 128])
    gi_in = ext("gi", [128, 64])
    hg_in = ext("hg", [128, 128])
    lbl_in = ext("lbl", [2, 128, 1024])
    invf_in = ext("invf", [128, 64])
    identf_in = ext("identf", [128, 128])
    md1_in = ext("md1", [128, 128])
    md2_in = ext("md2", [128, 128])
    md3_in = ext("md3", [128, 128])
    cind_in = ext("cind", [128, 2])
    cmask_in = ext("cmask", [128, 128])
    cbias_in = ext("cbias", [128, 2, 256])
    y = dt("y", [NOWN, D], F32, kind="ExternalOutput").ap()

    hT_d = dt("hT_d", [NTL, 128, KC * 128], BF16).ap()
    KT_d = dt("KT_d", [8, 128, NT], BF16).ap()
    V_d = dt("V_d", [8, NCH, 128, 16 * 128], BF16).ap()
    KI_d = dt("KI_d", [64, NT], BF16).ap()
    SN_d = dt("SN_d", [G, 128, 8 * 128], F32).ap()
    QT_d = dt("QT_d", [8, 128, NOWN], BF16).ap()
    QI_d = dt("QI_d", [8, 128, NOWN], BF16).ap()
    AG_d = dt("AG_d", [NOWN, 1024], BF16).ap()
    CT_d = dt("CT_d", [KC, 128, NOWN], BF16).ap()

    from contextlib import ExitStack
    P = Prog(nc)
    with ExitStack() as es:
        P.stack = es
        P.setup()

        def flush():
            P.barrier()
            with nc.Block() as block:
                P.emit(block)
        dr = {n: T(None, n) for n in ("hT", "KT", "V", "KI", "SN", "QT", "QI", "AG", "CT", "y")}
        for v_ in dr.values():
            v_.b.noww = True
        cnt = [0]

        def sb(shape, dtype, name=None, stack=None):
            cnt[0] += 1
            nm = f"{name or 't'}_{cnt[0]}"
            return T((stack or es).enter_context(nc.sbuf_tensor(nm, list(shape), dtype)), nm)

        def ps(shape, dtype, name=None, stack=None):
            cnt[0] += 1
            nm = f"{name or 'p'}_{cnt[0]}"
            return T((stack or es).enter_context(nc.psum_tensor(nm, list(shape), dtype)), nm)

        def load_const(src, shape, dtype=F32, name="c", q="sp"):
            t = sb(shape, dtype, name)
            P.dma(q, t[:], src, t, W=[t])
            return t

        identf = load_const(identf_in[:, :], [128, 128], name="identf")
        identb = sb([128, 128], BF16, "identb")
        P.op("dve", lambda e: e.tensor_copy(out=identb[:], in_=identf[:]), R=[identf], W=[identb])
        md1 = load_const(md1_in[:, :], [128, 128], name="md1")
        md2 = load_const(md2_in[:, :], [128, 128], name="md2")
        md3 = load_const(md3_in[:, :], [128, 128], name="md3")
        cind = load_const(cind_in[:, :], [128, 2], name="cind")
        cmask = load_const(cmask_in[:, :], [128, 128], name="cmask")
        invf = load_const(invf_in[:, :], [128, 64], name="invf")
        gq = load_const(gq_in[:, :], [128, 128], name="gq")
        gk = load_const(gk_in[:, :], [128, 128], name="gk")
        gi = load_const(gi_in[:, :], [128, 64], name="gi")
        hg = load_const(hg_in[:, :], [128, 128], name="hg")
        posi = load_const(posa[:, :], [128, NTL], I32, name="posi")
        posf = sb([128, NTL], F32, "posf")
        P.op("dve", lambda e: e.tensor_copy(out=posf[:], in_=posi[:]), R=[posi], W=[posf])
        validt = load_const(valid[:, :], [128, NTL], name="valid")
        Amod = sb([128, KC], F32, "Amod")
        Bmod = sb([128, KC], F32, "Bmod")
        grow = sb([1, D], F32, "grow")
        wsc = sb([128, NO, 16], F32, "wsc")
        ones1 = sb([1, 128], F32, "ones1")
        P.op("dve", lambda e: e.memset(ones1[:], 1.0), W=[ones1])

        s0 = ExitStack()
        if STOP >= 0:
            sc = sb([128, KC, 2], F32, "sc", s0)
            cin = sb([128, KC], F32, "cin", s0)
            P.dma("sp", cin[:], cfm[:, :], cin, W=[cin])
            for i2 in range(2):
                P.op("act", lambda e, i2=i2: e.activation(out=sc[:, :, i2], in_=cin[:], func=AF.Silu), R=[cin], W=[sc])
            adb = sb([128, 48], F32, "adb", s0)
            P.dma("sp", adb[:], adab_fm[:, :], adb, W=[adb])
            adg = sb([1, D], F32, "adg", s0)
            P.dma("sp", adg[:], adab_g[:, :], adg, W=[adg])
            ngf = sb([128, KC], F32, "ngf", s0)
            P.dma("sp", ngf[:], ng_fm[:, :], ngf, W=[ngf])
            awb = [sb([128, KC, 512], F32, "awb", s0) for _ in range(2)]
            mods_b = ps([128, 512], F32, "mods", s0)
            mods = mods_b.view(mods_b[:, 0:64].rearrange("p (c t) -> p c t", t=2))
            gps = ps([128, 512], F32, "gps", s0)
            awv = ada_w.rearrange("(kc p) n -> p kc n", p=128)
            for nb in range(12):
                a = awb[nb % 2]
                for hh in range(2):
                    P.dma("sp", a[:, hh * 8:(hh + 1) * 8, :], awv[:, hh * 8:(hh + 1) * 8, nb * 512:(nb + 1) * 512],
                          a, W=[a])
                if nb < 8:
                    for jj in range(4):
                        col = nb * 4 + jj
                        for kc in range(KC):
                            P.op("pe", lambda e, a=a, col=col, kc=kc, jj=jj: e.matmul(
                                mods[:, col, :], lhsT=a[:, kc, jj * 128:(jj + 1) * 128], rhs=sc[:, kc, :],
                                start=(kc == 0), stop=(kc == KC - 1)), R=[a, sc], W=[mods])
                else:
                    gb = nb - 8
                    for kc in range(KC):
                        P.op("pe", lambda e, a=a, kc=kc: e.matmul(
                            gps[0:1, :], lhsT=sc[:, kc, 0:1], rhs=a[:, kc, :],
                            start=(kc == 0), stop=(kc == KC - 1)), R=[a, sc], W=[gps])
                    P.op("dve", lambda e, gb=gb: e.tensor_tensor(
                        out=grow[0:1, gb * 512:(gb + 1) * 512], in0=gps[0:1, :],
                        in1=adg[0:1, gb * 512:(gb + 1) * 512], op=ALU.add), R=[gps, adg], W=[grow])
            P.op("dve", lambda e: e.tensor_tensor(out=Bmod[:], in0=mods[:, 0:16, 0], in1=adb[:, 0:16], op=ALU.add),
                 R=[mods, adb], W=[Bmod])
            P.op("dve", lambda e: e.tensor_tensor(out=Amod[:], in0=mods[:, 16:32, 0], in1=adb[:, 16:32], op=ALU.add),
                 R=[mods, adb], W=[Amod])
            P.op("dve", lambda e: e.scalar_tensor_tensor(out=Amod[:], in0=Amod[:], scalar=1.0, in1=ngf[:],
                                                        op0=ALU.add, op1=ALU.mult), R=[Amod, ngf], W=[Amod])

        flush(); s0.close()
        SINMAX = 3.1415925

        def evens(ap2d):
            return ap2d.rearrange("p (d two) -> p d two", two=2)[:, :, 0]

        def rope_tables(col, cs_out, tmp, ni):
            ang, nn, t2 = tmp[:, 0, :], tmp[:, 1, :], tmp[:, 2, :]
            pcol = posf[:, col:col + 1]
            P.op("dve", lambda e: e.tensor_scalar(out=ang, in0=invf[:], scalar1=pcol, scalar2=None, op0=ALU.mult),
                 R=[invf, posf], W=[tmp])
            P.op("dve", lambda e: e.tensor_scalar(out=nn, in0=ang, scalar1=1.0 / (2 * PI), scalar2=None, op0=ALU.mult),
                 R=[tmp], W=[tmp])
            P.op("dve", lambda e: e.tensor_copy(out=ni[:], in_=nn), R=[tmp], W=[ni])
            P.op("dve", lambda e: e.tensor_copy(out=nn, in_=ni[:]), R=[ni], W=[tmp])
            P.op("dve", lambda e: e.scalar_tensor_tensor(out=ang, in0=nn, scalar=-C1, in1=ang, op0=ALU.mult, op1=ALU.add),
                 R=[tmp], W=[tmp])
            P.op("dve", lambda e: e.scalar_tensor_tensor(out=ang, in0=nn, scalar=-C2, in1=ang, op0=ALU.mult, op1=ALU.add),
                 R=[tmp], W=[tmp])
            P.op("dve", lambda e: e.tensor_scalar(out=t2, in0=ang, scalar1=PI, scalar2=-2 * PI, op0=ALU.is_gt, op1=ALU.mult),
                 R=[tmp], W=[tmp])
            P.op("dve", lambda e: e.tensor_tensor(out=ang, in0=ang, in1=t2, op=ALU.add), R=[tmp], W=[tmp])
            P.op("dve", lambda e: e.tensor_scalar(out=t2, in0=ang, scalar1=-PI, scalar2=2 * PI, op0=ALU.is_lt, op1=ALU.mult),
                 R=[tmp], W=[tmp])
            P.op("dve", lambda e: e.tensor_tensor(out=ang, in0=ang, in1=t2, op=ALU.add), R=[tmp], W=[tmp])
            P.op("dve", lambda e: e.tensor_scalar(out=ang, in0=ang, scalar1=SINMAX, scalar2=-SINMAX, op0=ALU.min, op1=ALU.max),
                 R=[tmp], W=[tmp])
            P.op("act", lambda e: e.activation(out=cs_out[:, 1, :], in_=ang, func=AF.Sin), R=[tmp], W=[cs_out])
            P.op("act", lambda e: e.activation(out=t2, in_=ang, func=AF.Abs), R=[tmp], W=[tmp])
            P.op("act", lambda e: e.activation(out=cs_out[:, 0, :], in_=t2, func=AF.Sin, bias=halfpi[:, 0:1], scale=-1.0),
                 R=[tmp, halfpi], W=[cs_out])

        def gain_tables(cs, g, tab, half, sub):
            cos, sin = cs[:, 0, :], cs[:, 1, :]
            if sub:
                cos, sin = evens(cos), evens(sin)
            if g is None:
                for i, a in enumerate((cos, sin, cos, sin)):
                    P.op("dve", lambda e, i=i, a=a: e.tensor_copy(out=tab[:, i, :], in_=a), R=[cs], W=[tab])
                return
            g1, g2 = g[:, 0:half], g[:, half:2 * half]
            for i, (a, b_) in enumerate(((cos, g1), (sin, g2), (cos, g2), (sin, g1))):
                P.op("dve", lambda e, i=i, a=a, b_=b_: e.tensor_tensor(out=tab[:, i, :], in0=a, in1=b_, op=ALU.mult),
                     R=[cs, g], W=[tab])

        def rope_apply(eng, src, dst, nh, half, tab, tmp):
            def bc(i):
                return tab[:, i, :].unsqueeze(1).broadcast_to([128, nh, half])
            x1, x2 = src[:, :, 0:half], src[:, :, half:2 * half]
            ta = tmp[:, 0, 0:nh * half].rearrange("p (h d) -> p h d", h=nh)
            tb = tmp[:, 1, 0:nh * half].rearrange("p (h d) -> p h d", h=nh)
            o1, o2 = dst[:, :, 0:half], dst[:, :, half:2 * half]
            P.op(eng, lambda e: e.tensor_tensor(out=ta, in0=x1, in1=bc(0), op=ALU.mult), R=[src, tab], W=[tmp])
            P.op(eng, lambda e: e.tensor_tensor(out=tb, in0=x2, in1=bc(1), op=ALU.mult), R=[src, tab], W=[tmp])
            P.op(eng, lambda e: e.tensor_tensor(out=o1, in0=ta, in1=tb, op=ALU.subtract), R=[tmp], W=[dst])
            P.op(eng, lambda e: e.tensor_tensor(out=ta, in0=x2, in1=bc(2), op=ALU.mult), R=[src, tab], W=[tmp])
            P.op(eng, lambda e: e.tensor_tensor(out=tb, in0=x1, in1=bc(3), op=ALU.mult), R=[src, tab], W=[tmp])
            P.op(eng, lambda e: e.tensor_tensor(out=o2, in0=ta, in1=tb, op=ALU.add), R=[tmp], W=[dst])

        def rstd_of(ss, n, width):
            P.op("act", lambda e: e.activation(out=ss[:, 0:n], in_=ss[:, 0:n], func=AF.Sqrt, bias=epsc[:, 0:1],
                                               scale=1.0 / width), R=[ss, epsc], W=[ss])
            P.op("dve", lambda e: e.reciprocal(out=ss[:, 0:n], in_=ss[:, 0:n]), R=[ss], W=[ss])

        halfpi = sb([128, 1], F32, "halfpi")
        P.op("dve", lambda e: e.memset(halfpi[:], PI / 2 - 1e-6), W=[halfpi])
        epsc = sb([128, 1], F32, "epsc")
        P.op("dve", lambda e: e.memset(epsc[:], EPS), W=[epsc])

        def load_w_cols(dst, dcol, c0, ncols, stg, cast_i):
            for kc in range(KC):
                s = stg[cast_i[0] % len(stg)]
                P.dma("sp", s[:, 0:ncols], w_in[kc * 128:(kc + 1) * 128, c0:c0 + ncols], s, W=[s])
                en = ("pool", "dve", "act")[cast_i[0] % 3]
                if en == "act":
                    P.op("act", lambda e, s=s, kc=kc: e.copy(out=dst[:, kc, dcol:dcol + ncols], in_=s[:, 0:ncols]),
                         R=[s], W=[dst])
                else:
                    P.op(en, lambda e, s=s, kc=kc: e.tensor_copy(out=dst[:, kc, dcol:dcol + ncols], in_=s[:, 0:ncols]),
                         R=[s], W=[dst])
                cast_i[0] += 1

        def project(hT, W, c0, ncols, pout):
            for kc in range(KC):
                P.op("pe", lambda e, kc=kc: e.matmul(pout[:, 0:ncols], lhsT=hT[:, kc, :], rhs=W[:, kc, c0:c0 + ncols],
                                                     start=(kc == 0), stop=(kc == KC - 1)), R=[hT, W], W=[pout])

        def qk_epilogue(pp, gtab, tmpr, ss, sq, dst, nh=4, half=64):
            width = 2 * half
            P.op("act", lambda e: e.activation(out=sq[:, 0:nh * width], in_=pp[:, 0:nh * width], func=AF.Square),
                 R=[pp], W=[sq])
            P.op("dve", lambda e: e.tensor_reduce(out=ss[:, 0:nh], in_=sq[:, 0:nh * width].rearrange("p (h d) -> p h d", h=nh),
                                                  axis=AX.X, op=ALU.add), R=[sq], W=[ss])
            rstd_of(ss, nh, width)
            src = pp.view(pp[:, 0:nh * width].rearrange("p (h d) -> p h d", h=nh))
            rv = sq.view(sq[:, 0:nh * width].rearrange("p (h d) -> p h d", h=nh))
            rope_apply("dve", src, rv, nh, half, gtab, tmpr)
            P.op("dve", lambda e: e.tensor_tensor(out=dst[:, :, :], in0=rv[:, :, :],
                                                  in1=ss[:, 0:nh].unsqueeze(2).broadcast_to([128, nh, width]),
                                                  op=ALU.mult), R=[sq, ss], W=[dst])

        sA = ExitStack()
        if STOP >= 1:
            WA = sb([128, KC, 2112], BF16, "WA", sA)
            stg = [sb([128, 1024], F32, "stg", sA) for _ in range(3)]
            ci = [0]
            load_w_cols(WA, 0, O_AK, 1024, stg, ci)
            load_w_cols(WA, 1024, O_AV, 1024, stg, ci)
            load_w_cols(WA, 2048, O_IK, 64, stg, ci)
            xt = [sb([128, D], F32, "xt", sA) for _ in range(2)]
            junk = sb([128, D], BF16, "junk", sA)
            xns = [sb([128, D], BF16, "xn", sA) for _ in range(2)]
            ssx = [sb([128, 2], F32, "ssx", sA) for _ in range(2)]
            hTs = [sb([128, KC, 128], BF16, "hT", sA) for _ in range(2)]
            tp = [ps([128, 8, 128], BF16, "tp", sA) for _ in range(2)]
            pj = [ps([128, 512], F32, "pj", sA) for _ in range(3)]
            ktp = ps([128, 8, 128], BF16, "ktp", sA)
            kip = ps([128, 1024], BF16, "kip", sA)
            cs = sb([128, 2, 64], F32, "cs", sA)
            rtmp = sb([128, 3, 64], F32, "rtmp", sA)
            rni = sb([128, 64], I32, "rni", sA)
            ktab = sb([128, 4, 64], F32, "ktab", sA)
            itab = sb([128, 4, 32], F32, "itab", sA)
            tmpr = sb([128, 2, 256], F32, "tmpr", sA)
            ssk = sb([128, 8], F32, "ssk", sA)
            sq = sb([128, 512], F32, "sq", sA)
            kn = sb([128, 8, 128], BF16, "kn", sA)
            kin = sb([128, 1, 64], BF16, "kin", sA)
            KTs = [sb([128, 8, 512], BF16, "KTs", sA) for _ in range(2)]
            KIs = [sb([64, 512], BF16, "KIs", sA) for _ in range(2)]
            Vs = [sb([128, 8, 128], BF16, "Vs", sA) for _ in range(2)]
            pji = 0
            def prep(T_):
                x_ = xt[T_ % 2]
                xn = xns[T_ % 2]
                s_ = ssx[T_ % 2]
                h_ = hTs[T_ % 2]
                P.dma("sp", x_[:], xa[T_ * 128:(T_ + 1) * 128, :], x_, W=[x_])
                P.op("act", lambda e, x_=x_, s_=s_: e.activation(out=junk[:], in_=x_[:], func=AF.Square,
                                                              accum_out=s_[:, 0:1]), R=[x_], W=[junk, s_])
                P.op("act", lambda e, s_=s_: e.activation(out=s_[:, 0:1], in_=s_[:, 0:1], func=AF.Sqrt, bias=epsc[:, 0:1],
                                                       scale=1.0 / D), R=[s_, epsc], W=[s_])
                P.op("dve", lambda e, s_=s_: e.reciprocal(out=s_[:, 0:1], in_=s_[:, 0:1]), R=[s_], W=[s_])
                P.op("dve", lambda e, x_=x_, s_=s_: e.tensor_scalar(out=xn[:], in0=x_[:], scalar1=s_[:, 0:1], scalar2=None,
                                                                 op0=ALU.mult), R=[x_, s_], W=[xn])
                for hf in range(2):
                    t_ = tp[hf]
                    for k8 in range(8):
                        kc = hf * 8 + k8
                        P.op("pe", lambda e, t_=t_, k8=k8, kc=kc: e.transpose(t_[:, k8, :], xn[:, kc * 128:(kc + 1) * 128],
                                                                         identb[:]), R=[xn, identb], W=[t_])
                    for k8 in range(8):
                        kc = hf * 8 + k8
                        if k8 % 2 == 0:
                            P.op("act", lambda e, t_=t_, k8=k8, kc=kc, h_=h_: e.activation(
                                out=h_[:, kc, :], in_=t_[:, k8, :], func=AF.Identity, bias=Bmod[:, kc:kc + 1],
                                scale=Amod[:, kc:kc + 1]), R=[t_, Amod, Bmod], W=[h_])
                        else:
                            P.op("dve", lambda e, t_=t_, k8=k8, kc=kc, h_=h_: e.tensor_scalar(
                                out=h_[:, kc, :], in0=t_[:, k8, :], scalar1=Amod[:, kc:kc + 1], scalar2=Bmod[:, kc:kc + 1],
                                op0=ALU.mult, op1=ALU.add), R=[t_, Amod, Bmod], W=[h_])
                P.dma("pool", hT_d[T_], h_[:].rearrange("p k t -> p (k t)"), h_, R=[h_], W=[dr["hT"]])
            prep(0)
            for T_ in range(NTL):
                h_ = hTs[T_ % 2]
                if T_ + 1 < NTL:
                    prep(T_ + 1)
                rope_tables(T_, cs, rtmp, rni)
                gain_tables(cs, gk, ktab, 64, False)
                gain_tables(cs, gi, itab, 32, True)
                slot = (T_ // 4) % 2
                KT_ = KTs[slot]
                KI_ = KIs[slot]
                t4 = T_ % 4
                for gidx in range(2):
                    pp = pj[pji % 3]; pji += 1
                    project(h_, WA, gidx * 512, 512, pp)
                    knv = kn.view(kn[:, gidx * 4:(gidx + 1) * 4, :])
                    qk_epilogue(pp, ktab, tmpr, ssk, sq, knv)
                for hd in range(8):
                    P.op("pe", lambda e, hd=hd: e.transpose(ktp[:, hd, :], kn[:, hd, :], identb[:]), R=[kn, identb], W=[ktp])
                P.op("act", lambda e, KT_=KT_, t4=t4: e.copy(out=KT_[:, :, t4 * 128:(t4 + 1) * 128], in_=ktp[:, :, :]),
                     R=[ktp], W=[KT_])
                V_ = Vs[T_ % 2]
                for gidx in range(2):
                    pp = pj[pji % 3]; pji += 1
                    project(h_, WA, 1024 + gidx * 512, 512, pp)
                    P.op("act", lambda e, pp=pp, gidx=gidx, V_=V_: e.copy(
                        out=V_[:, gidx * 4:(gidx + 1) * 4, :], in_=pp[:, :].rearrange("p (h d) -> p h d", h=4)),
                        R=[pp], W=[V_])
                ch, blk = T_ // 16, T_ % 16
                P.dma("pool", V_d[:, ch, :, blk * 128:(blk + 1) * 128].rearrange("h p d -> p h d"), V_[:], V_,
                      R=[V_], W=[dr["V"]])
                pp = pj[pji % 3]; pji += 1
                project(h_, WA, 2048, 64, pp)
                kiv = kin.view(kin[:, 0:1, :])
                qk_epilogue(pp, itab, tmpr, ssk, sq, kiv, nh=1, half=32)
                P.op("pe", lambda e: e.transpose(kip[0:64, 0:128], kin[:, 0, :], identb[:]), R=[kin, identb], W=[kip])
                P.op("act", lambda e, KI_=KI_, t4=t4: e.copy(out=KI_[:, t4 * 128:(t4 + 1) * 128], in_=kip[0:64, 0:128]),
                     R=[kip], W=[KI_])
                if t4 == 3 or T_ == NTL - 1:
                    nt = t4 + 1
                    t0 = (T_ // 4) * 4 * 128
                    P.dma("pool", KT_d[:, :, t0:t0 + nt * 128].rearrange("h d t -> d h t"), KT_[:, :, 0:nt * 128], KT_,
                          R=[KT_], W=[dr["KT"]])
                    P.dma("pool", KI_d[:, t0:t0 + nt * 128], KI_[:, 0:nt * 128], KI_, R=[KI_], W=[dr["KI"]])

        flush(); sA.close()
        sLB = ExitStack()
        lb = sb([128, 1024], F32, "lb", sLB)
        oml = sb([128, 1024], F32, "oml", sLB)
        sL = ExitStack()
        if STOP >= 2:
            l0 = sb([128, 1024], F32, "l0", sL)
            l1 = sb([128, 1024], F32, "l1", sL)
            P.dma("sp", l0[:], lbl_in[0], l0, W=[l0])
            P.dma("sp", l1[:], lbl_in[1], l1, W=[l1])
            P.op("dve", lambda e: e.tensor_tensor(out=l0[:], in0=l0[:], in1=l1[:], op=ALU.subtract), R=[l0, l1], W=[l0])
            P.op("act", lambda e: e.activation(out=lb[:], in_=l0[:], func=AF.Sigmoid), R=[l0], W=[lb])
            P.op("dve", lambda e: e.tensor_scalar(out=oml[:], in0=lb[:], scalar1=-1.0, scalar2=1.0, op0=ALU.mult, op1=ALU.add),
                 R=[lb], W=[oml])

        flush(); sL.close()
        sB = ExitStack()
        if STOP >= 2:
            WB = sb([128, KC, 2048], BF16, "WB", sB)
            stg = [sb([128, 1024], F32, "stg", sB) for _ in range(3)]
            ci = [0]
            load_w_cols(WB, 0, O_RF, 1024, stg, ci)
            load_w_cols(WB, 1024, O_RI, 1024, stg, ci)
            hTs = [sb([128, KC, 128], BF16, "hT", sB) for _ in range(3)]
            pj = [ps([128, 512], F32, "pj", sB) for _ in range(4)]
            d2p = [ps([128, 512], F32, "d2p", sB) for _ in range(2)]
            csp = [ps([128, 4, 128], F32, "csp", sB) for _ in range(2)]
            sig = sb([128, 1024], F32, "sig", sB)
            logf = sb([128, 1024], F32, "logf", sB)
            kk = sb([128, 1024], F32, "kk", sB)
            E2 = sb([128, 1024], F32, "E2", sB)
            Kl = sb([128, 1024], BF16, "Kl", sB)
            vv = sb([128, 1024], BF16, "vv", sB)
            dec = sb([128, 8, 2], F32, "dec", sB)
            S = sb([128, 8, 128], F32, "S", sB)
            P.op("dve", lambda e: e.memset(S[:], 0.0), W=[S])
            own_tiles = {16 + 16 * g: g for g in range(G)}
            last_needed = 16 + 16 * (G - 1)
            for T_ in range(last_needed + 1):
                if T_ in own_tiles:
                    g = own_tiles[T_]
                    P.dma("pool", SN_d[g], S[:].rearrange("p h e -> p (h e)"), S, R=[S], W=[dr["SN"]])
                if T_ == last_needed:
                    break
                h_ = hTs[T_ % 3]
                P.dma("sp", h_[:].rearrange("p k t -> p (k t)"), hT_d[T_], h_, R=[dr["hT"]], W=[h_])
                for gidx in range(2):
                    project(h_, WB, gidx * 512, 512, pj[gidx])
                    P.op("act", lambda e, gidx=gidx: e.activation(out=sig[:, gidx * 512:(gidx + 1) * 512], in_=pj[gidx][:, :],
                                                                  func=AF.Sigmoid), R=[pj[gidx]], W=[sig])
                for gidx in range(2):
                    project(h_, WB, 1024 + gidx * 512, 512, pj[2 + gidx])
                    P.op("act", lambda e, gidx=gidx, T_=T_: e.activation(
                        out=vv[:, gidx * 512:(gidx + 1) * 512], in_=pj[2 + gidx][:, :], func=AF.Copy,
                        scale=validt[:, T_:T_ + 1]), R=[pj[2 + gidx], validt], W=[vv])
                P.op("dve", lambda e: e.tensor_tensor(out=sig[:], in0=sig[:], in1=oml[:], op=ALU.mult), R=[sig, oml], W=[sig])
                P.op("dve", lambda e: e.tensor_tensor(out=sig[:], in0=sig[:], in1=lb[:], op=ALU.add), R=[sig, lb], W=[sig])
                P.op("act", lambda e: e.activation(out=logf[:], in_=sig[:], func=AF.Ln), R=[sig], W=[logf])
                P.op("dve", lambda e: e.tensor_scalar(out=kk[:], in0=sig[:], scalar1=-1.0, scalar2=1.0, op0=ALU.mult, op1=ALU.add),
                     R=[sig], W=[kk])
                for gidx in range(2):
                    P.op("pe", lambda e, gidx=gidx: e.matmul(d2p[gidx][:, :], lhsT=md2[:], rhs=logf[:, gidx * 512:(gidx + 1) * 512],
                                                             start=True, stop=True), R=[md2, logf], W=[d2p[gidx]])
                    P.op("act", lambda e, gidx=gidx: e.activation(out=E2[:, gidx * 512:(gidx + 1) * 512], in_=d2p[gidx][:, :],
                                                                  func=AF.Exp), R=[d2p[gidx]], W=[E2])
                P.op("dve", lambda e: e.tensor_tensor(out=Kl[:], in0=kk[:], in1=E2[:], op=ALU.mult), R=[kk, E2], W=[Kl])
                dp = d2p[0]
                for hd in range(8):
                    P.op("pe", lambda e, hd=hd: e.matmul(dp[:, hd * 2:hd * 2 + 2], lhsT=logf[:, hd * 128:(hd + 1) * 128],
                                                         rhs=cind[:, :], start=True, stop=True), R=[logf, cind], W=[dp])
                P.op("act", lambda e: e.activation(out=dec[:].rearrange("p h c -> p (h c)"), in_=dp[:, 0:16], func=AF.Exp),
                     R=[dp], W=[dec])
                for cj in range(2):
                    for hg_ in range(2):
                        cp = csp[hg_]
                        for h4 in range(4):
                            hd = hg_ * 4 + h4
                            P.op("pe", lambda e, cp=cp, h4=h4, hd=hd, cj=cj: e.matmul(
                                cp[:, h4, :], lhsT=Kl[cj * 64:(cj + 1) * 64, hd * 128:(hd + 1) * 128],
                                rhs=vv[cj * 64:(cj + 1) * 64, hd * 128:(hd + 1) * 128], start=True, stop=True),
                                R=[Kl, vv], W=[cp])
                        for h4 in range(4):
                            hd = hg_ * 4 + h4
                            P.op("dve", lambda e, cp=cp, h4=h4, hd=hd, cj=cj: e.scalar_tensor_tensor(
                                out=S[:, hd, :], in0=S[:, hd, :], scalar=dec[:, hd, cj:cj + 1], in1=cp[:, h4, :],
                                op0=ALU.mult, op1=ALU.add), R=[S, dec, cp], W=[S])

        flush(); sB.close()
        sC = ExitStack()
        if STOP >= 3:
            hTo = sb([128, KC, NOWN], BF16, "hTo", sC)
            for j in range(NO):
                T_ = 16 + 16 * (j // 2) + (j % 2)
                P.dma("sp", hTo[:, :, j * 128:(j + 1) * 128], hT_d[T_].rearrange("p (k t) -> p k t", k=KC), hTo,
                      R=[dr["hT"]], W=[hTo])
            stg = [sb([128, 512], F32, "stg", sC) for _ in range(4)]
            Wb = [sb([128, KC, 512], BF16, "Wb", sC) for _ in range(2)]
            pj = [ps([128, 512], F32, "pj", sC) for _ in range(2)]
            tpb = [ps([128, 8, 128], BF16, "tpb", sC) for _ in range(2)]
            dps = ps([128, 512], F32, "dps", sC)
            atp = ps([128, 512], F32, "atp", sC)
            op_ = ps([128, 512], F32, "op", sC)
            csp = ps([128, 4, 128], F32, "csp", sC)
            cso = sb([128, NO, 2, 64], F32, "cso", sC)
            rtmp = sb([128, 3, 64], F32, "rtmp", sC)
            rni = sb([128, 64], I32, "rni", sC)
            qtab = sb([128, 4, 64], F32, "qtab", sC)
            itab = sb([128, 4, 32], F32, "itab", sC)
            ones64 = sb([128, 64], F32, "ones64", sC)
            P.op("dve", lambda e: e.memset(ones64[:], 1.0), W=[ones64])
            tmpr = sb([128, 2, 256], F32, "tmpr", sC)
            ssk = sb([128, 8], F32, "ssk", sC)
            sq = sb([128, 512], F32, "sq", sC)
            qn = sb([128, 4, 128], BF16, "qn", sC)
            stq = [sb([128, 4, 128], BF16, "stq", sC) for _ in range(2)]
            agt = [sb([128, 512], BF16, "agt", sC) for _ in range(2)]
            csv = [cso.view(cso[:, j, :, :]) for j in range(NO)]
            for j in range(NO):
                T_ = 16 + 16 * (j // 2) + (j % 2)
                rope_tables(T_, csv[j], rtmp, rni)
            ci = [0]

            def load_block(wb, cols):
                off = 0
                for (c0, n) in cols:
                    for kc in range(KC):
                        s = stg[ci[0] % 4]
                        P.dma("sp", s[:, 0:n], w_in[kc * 128:(kc + 1) * 128, c0:c0 + n], s, W=[s])
                        en = ("pool", "pool", "act")[ci[0] % 3]
                        if en == "act":
                            P.op("act", lambda e, s=s, kc=kc, off=off, n=n: e.copy(out=wb[:, kc, off:off + n], in_=s[:, 0:n]),
                                 R=[s], W=[wb])
                        else:
                            P.op(en, lambda e, s=s, kc=kc, off=off, n=n: e.tensor_copy(out=wb[:, kc, off:off + n], in_=s[:, 0:n]),
                                 R=[s], W=[wb])
                        ci[0] += 1
                    off += n

            bi = 0
            if STOP >= 3.2:
                for qb in range(2):
                    wb = Wb[bi % 2]; bi += 1
                    load_block(wb, [(O_AQ + qb * 512, 512)])
                    for j in range(NO):
                        pp = pj[j % 2]
                        for kc in range(KC):
                            P.op("pe", lambda e, kc=kc, j=j, pp=pp, wb=wb: e.matmul(
                                pp[:, :], lhsT=hTo[:, kc, j * 128:(j + 1) * 128], rhs=wb[:, kc, 0:512],
                                start=(kc == 0), stop=(kc == KC - 1)), R=[hTo, wb], W=[pp])
                        gain_tables(csv[j], gq, qtab, 64, False)
                        qk_epilogue(pp, qtab, tmpr, ssk, sq, qn.view(qn[:, :, :]))
                        tq = tpb[j % 2]
                        for h4 in range(4):
                            P.op("pe", lambda e, h4=h4, tq=tq: e.transpose(tq[:, h4, :], qn[:, h4, :], identb[:]),
                                 R=[qn, identb], W=[tq])
                        sq_ = stq[j % 2]
                        P.op("act", lambda e, sq_=sq_, tq=tq: e.copy(out=sq_[:], in_=tq[:, 0:4, :]), R=[tq], W=[sq_])
                        P.dma("pool", QT_d[qb * 4:(qb + 1) * 4, :, j * 128:(j + 1) * 128].rearrange("h d t -> d h t"),
                              sq_[:], sq_, R=[sq_], W=[dr["QT"]])
            if STOP >= 3.3:
                for gb in range(2):
                    wb = Wb[bi % 2]; bi += 1
                    load_block(wb, [(O_AG + gb * 512, 512)])
                    for j in range(NO):
                        pp = pj[j % 2]
                        for kc in range(KC):
                            P.op("pe", lambda e, kc=kc, j=j, pp=pp, wb=wb: e.matmul(
                                pp[:, :], lhsT=hTo[:, kc, j * 128:(j + 1) * 128], rhs=wb[:, kc, 0:512],
                                start=(kc == 0), stop=(kc == KC - 1)), R=[hTo, wb], W=[pp])
                        a_ = agt[j % 2]
                        P.op("act", lambda e, a_=a_, pp=pp: e.activation(out=a_[:], in_=pp[:, :], func=AF.Silu), R=[pp], W=[a_])
                        P.dma("pool", AG_d[j * 128:(j + 1) * 128, gb * 512:(gb + 1) * 512], a_[:], a_, R=[a_], W=[dr["AG"]])
            if STOP >= 3.4:
                for ib in range(2):
                    wb = Wb[bi % 2]; bi += 1
                    load_block(wb, [(O_IQ + ib * 512, 512)])
                    for j in range(NO):
                        pp = pj[j % 2]
                        for kc in range(KC):
                            P.op("pe", lambda e, kc=kc, j=j, pp=pp, wb=wb: e.matmul(
                                pp[:, :], lhsT=hTo[:, kc, j * 128:(j + 1) * 128], rhs=wb[:, kc, 0:512],
                                start=(kc == 0), stop=(kc == KC - 1)), R=[hTo, wb], W=[pp])
                        gain_tables(csv[j], None, itab, 32, True)
                        src = pp.view(pp[:, :].rearrange("p (h d) -> p h d", h=8))
                        qiv = qn.view(qn[:].rearrange("p a (b d) -> p (a b) d", b=2))
                        rope_apply("dve", src, qiv, 8, 32, itab, tmpr)
                        tq = tpb[j % 2]
                        for h4 in range(4):
                            P.op("pe", lambda e, h4=h4, tq=tq: e.transpose(tq[:, h4, :], qn[:, h4, :], identb[:]),
                                 R=[qn, identb], W=[tq])
                        sq_ = stq[j % 2]
                        P.op("act", lambda e, sq_=sq_, tq=tq: e.copy(out=sq_[:], in_=tq[:, 0:4, :]), R=[tq], W=[sq_])
                        P.dma("pool", QI_d[ib * 4:(ib + 1) * 4, :, j * 128:(j + 1) * 128].rearrange("h d t -> d h t"),
                              sq_[:], sq_, R=[sq_], W=[dr["QI"]])
            if STOP >= 3.5:
                wb = Wb[bi % 2]; bi += 1
                load_block(wb, [(O_IW, 16)])
                for j in range(NO):
                    pp = pj[j % 2]
                    for kc in range(KC):
                        P.op("pe", lambda e, kc=kc, j=j, pp=pp, wb=wb: e.matmul(
                            pp[:, 0:16], lhsT=hTo[:, kc, j * 128:(j + 1) * 128], rhs=wb[:, kc, 0:16],
                            start=(kc == 0), stop=(kc == KC - 1)), R=[hTo, wb], W=[pp])
                    P.op("dve", lambda e, j=j, pp=pp: e.tensor_scalar(out=wsc[:, j, :], in0=pp[:, 0:16], scalar1=0.25 * 0.125,
                                                                   scalar2=None, op0=ALU.mult), R=[pp], W=[wsc])
            Srun = sb([128, 8, 128], F32, "Srun", sC)
            Sb = [sb([128, 128], BF16, "Sb", sC) for _ in range(2)]
            S1 = sb([128, 128], F32, "S1", sC)
            qs = sb([128, 128], F32, "qs", sC)
            gs = sb([128, 128], F32, "gs", sC)
            sg2 = sb([128, 128], F32, "sg2", sC)
            lf = sb([128, 128], F32, "lf", sC)
            k2 = sb([128, 128], F32, "k2", sC)
            v2 = sb([128, 128], BF16, "v2", sC)
            Ex = sb([128, 4, 128], F32, "Ex", sC)
            Qd = sb([128, 128], BF16, "Qd", sC)
            Kd = sb([128, 128], BF16, "Kd", sC)
            Kl2 = sb([128, 128], F32, "Kl2", sC)
            Kz = sb([128, 2, 128], BF16, "Kz", sC)
            Qb = sb([128, 128], BF16, "Qb", sC)
            TT = sb([128, 4, 128], BF16, "TT", sC)
            P.op("dve", lambda e: e.memset(TT[:], 0.0), W=[TT])
            ATs = sb([128, 128], BF16, "ATs", sC)
            dec2 = sb([128, 2], F32, "dec2", sC)
            sso = sb([128, 2], F32, "sso", sC)
            orr = sb([128, 128], BF16, "orr", sC)
            orT = [sb([128, 128], BF16, "orT", sC) for _ in range(2)]
            for hd in range(8 if STOP >= 3.605 else 0):
                wb = Wb[bi % 2]; bi += 1
                load_block(wb, [(O_RQ + hd * 128, 128), (O_RF + hd * 128, 128), (O_RI + hd * 128, 128),
                                (O_RG + hd * 128, 128)])
                for j in range(NO):
                    g = j // 2
                    if j % 2 == 0:
                        P.dma("sp", Srun[:, hd, :], SN_d[g, :, hd * 128:(hd + 1) * 128], Srun, R=[dr["SN"]], W=[Srun])
                    pp = pj[j % 2]
                    for kc in range(KC):
                        P.op("pe", lambda e, kc=kc, j=j, pp=pp, wb=wb: e.matmul(
                            pp[:, :], lhsT=hTo[:, kc, j * 128:(j + 1) * 128], rhs=wb[:, kc, 0:512],
                            start=(kc == 0), stop=(kc == KC - 1)), R=[hTo, wb], W=[pp])
                    P.op("act", lambda e, pp=pp: e.activation(out=qs[:], in_=pp[:, 0:128], func=AF.Silu), R=[pp], W=[qs])
                    P.op("act", lambda e, pp=pp: e.activation(out=sg2[:], in_=pp[:, 128:256], func=AF.Sigmoid), R=[pp], W=[sg2])
                    P.op("act", lambda e, pp=pp: e.copy(out=v2[:], in_=pp[:, 256:384]), R=[pp], W=[v2])
                    P.op("act", lambda e, pp=pp: e.activation(out=gs[:], in_=pp[:, 384:512], func=AF.Silu), R=[pp], W=[gs])
                    P.op("dve", lambda e, hd=hd: e.tensor_tensor(out=sg2[:], in0=sg2[:], in1=oml[:, hd * 128:(hd + 1) * 128],
                                                                 op=ALU.mult), R=[sg2, oml], W=[sg2])
                    P.op("dve", lambda e, hd=hd: e.tensor_tensor(out=sg2[:], in0=sg2[:], in1=lb[:, hd * 128:(hd + 1) * 128],
                                                                 op=ALU.add), R=[sg2, lb], W=[sg2])
                    P.op("act", lambda e: e.activation(out=lf[:], in_=sg2[:], func=AF.Ln), R=[sg2], W=[lf])
                    P.op("dve", lambda e: e.tensor_scalar(out=k2[:], in0=sg2[:], scalar1=-1.0, scalar2=1.0, op0=ALU.mult,
                                                          op1=ALU.add), R=[sg2], W=[k2])
                    if STOP < 3.62:
                        continue
                    for i, m_ in enumerate((md1, md2, md3)):
                        P.op("pe", lambda e, i=i, m_=m_: e.matmul(dps[:, i * 128:(i + 1) * 128], lhsT=m_[:], rhs=lf[:],
                                                                  start=True, stop=True), R=[m_, lf], W=[dps])
                    P.op("pe", lambda e: e.matmul(dps[:, 384:386], lhsT=lf[:], rhs=cind[:, :], start=True, stop=True),
                         R=[lf, cind], W=[dps])
                    P.op("act", lambda e: e.activation(out=Ex[:, 0, :], in_=dps[:, 0:128], func=AF.Exp), R=[dps], W=[Ex])
                    P.op("act", lambda e: e.activation(out=Ex[:, 1, :], in_=dps[:, 0:128], func=AF.Exp, scale=-1.0),
                         R=[dps], W=[Ex])
                    P.op("act", lambda e: e.activation(out=Ex[:, 2, :], in_=dps[:, 128:256], func=AF.Exp), R=[dps], W=[Ex])
                    P.op("act", lambda e: e.activation(out=Ex[:, 3, :], in_=dps[:, 256:384], func=AF.Exp), R=[dps], W=[Ex])
                    P.op("act", lambda e: e.activation(out=dec2[:], in_=dps[:, 384:386], func=AF.Exp), R=[dps], W=[dec2])
                    P.op("dve", lambda e: e.tensor_tensor(out=Qd[:], in0=qs[:], in1=Ex[:, 0, :], op=ALU.mult), R=[qs, Ex], W=[Qd])
                    P.op("dve", lambda e: e.tensor_tensor(out=Kd[:], in0=k2[:], in1=Ex[:, 1, :], op=ALU.mult), R=[k2, Ex], W=[Kd])
                    P.op("dve", lambda e: e.tensor_tensor(out=Kl2[:], in0=k2[:], in1=Ex[:, 2, :], op=ALU.mult), R=[k2, Ex], W=[Kl2])
                    P.op("dve", lambda e: e.tensor_tensor(out=Qb[:], in0=qs[:], in1=Ex[:, 3, :], op=ALU.mult), R=[qs, Ex], W=[Qb])
                    if STOP < 3.63:
                        continue
                    tq = tpb[j % 2]
                    for i, s_ in enumerate((Qd, Kd, Qb)):
                        P.op("pe", lambda e, i=i, s_=s_, tq=tq: e.transpose(tq[:, i, :], s_[:], identb[:]),
                             R=[s_, identb], W=[tq])
                    P.op("act", lambda e, tq=tq: e.copy(out=TT[:, 0:2, :], in_=tq[:, 0:2, :]), R=[tq], W=[TT])
                    P.op("act", lambda e, tq=tq: e.copy(out=TT[:, 2, 0:64], in_=tq[:, 2, 0:64]), R=[tq], W=[TT])
                    P.op("act", lambda e, tq=tq: e.copy(out=TT[:, 3, 64:128], in_=tq[:, 2, 64:128]), R=[tq], W=[TT])
                    P.op("pe", lambda e: e.matmul(atp[:, 0:128], lhsT=TT[:, 1, :], rhs=TT[:, 0, :], start=True, stop=True),
                         R=[TT], W=[atp])
                    P.op("dve", lambda e: e.tensor_tensor(out=ATs[:], in0=atp[:, 0:128], in1=cmask[:], op=ALU.mult),
                         R=[atp, cmask], W=[ATs])
                    if STOP < 3.64:
                        continue
                    P.op("dve", lambda e, hd=hd: e.tensor_copy(out=Sb[0][:], in_=Srun[:, hd, :]), R=[Srun], W=[Sb[0]])
                    for cj in range(2):
                        P.op("dve", lambda e, cj=cj: e.tensor_scalar(out=Kz[:, cj, :], in0=Kl2[:], scalar1=cind[:, cj:cj + 1],
                                                                     scalar2=None, op0=ALU.mult), R=[Kl2, cind], W=[Kz])
                    for cj in range(2):
                        P.op("pe", lambda e, cj=cj: e.matmul(csp[:, cj, :], lhsT=Kz[:, cj, :], rhs=v2[:], start=True, stop=True),
                             R=[Kz, v2], W=[csp])
                    P.op("dve", lambda e, hd=hd: e.scalar_tensor_tensor(out=S1[:], in0=Srun[:, hd, :], scalar=dec2[:, 0:1],
                                                                        in1=csp[:, 0, :], op0=ALU.mult, op1=ALU.add),
                         R=[Srun, dec2, csp], W=[S1])
                    P.op("dve", lambda e: e.tensor_copy(out=Sb[1][:], in_=S1[:]), R=[S1], W=[Sb[1]])
                    P.op("dve", lambda e, hd=hd: e.scalar_tensor_tensor(out=Srun[:, hd, :], in0=S1[:], scalar=dec2[:, 1:2],
                                                                        in1=csp[:, 1, :], op0=ALU.mult, op1=ALU.add),
                         R=[S1, dec2, csp], W=[Srun])
                    if STOP < 3.65:
                        continue
                    P.op("pe", lambda e: e.matmul(op_[:, 0:128], lhsT=ATs[:], rhs=v2[:], start=True, stop=False),
                         R=[ATs, v2], W=[op_])
                    P.op("pe", lambda e: e.matmul(op_[:, 0:128], lhsT=TT[:, 2, :], rhs=Sb[0][:], start=False, stop=False),
                         R=[TT, Sb[0]], W=[op_])
                    P.op("pe", lambda e: e.matmul(op_[:, 0:128], lhsT=TT[:, 3, :], rhs=Sb[1][:], start=False, stop=True),
                         R=[TT, Sb[1]], W=[op_])
                    P.op("act", lambda e: e.activation(out=sq[:, 0:128], in_=op_[:, 0:128], func=AF.Square, accum_out=sso[:, 0:1]),
                         R=[op_], W=[sq, sso])
                    rstd_of(sso, 1, 128.0)
                    P.op("dve", lambda e: e.tensor_tensor(out=gs[:], in0=gs[:], in1=hg[:], op=ALU.mult), R=[gs, hg], W=[gs])
                    P.op("dve", lambda e: e.scalar_tensor_tensor(out=orr[:], in0=op_[:, 0:128], scalar=sso[:, 0:1], in1=gs[:],
                                                                 op0=ALU.mult, op1=ALU.mult), R=[op_, sso, gs], W=[orr])
                    if STOP < 3.66:
                        continue
                    P.op("pe", lambda e, tq=tq: e.transpose(tq[:, 3, :], orr[:], identb[:]), R=[orr, identb], W=[tq])
                    o_ = orT[j % 2]
                    P.op("act", lambda e, o_=o_, tq=tq: e.copy(out=o_[:], in_=tq[:, 3, :]), R=[tq], W=[o_])
                    P.dma("pool", CT_d[8 + hd, :, j * 128:(j + 1) * 128], o_[:], o_, R=[o_], W=[dr["CT"]])

        flush(); sC.close(); sLB.close()
        sD = ExitStack()
        if STOP >= 4:
            kiT = sb([128, NT], BF16, "kiT", sD)
            P.dma("sp", kiT[0:64, :], KI_d[:, :], kiT, R=[dr["KI"]], W=[kiT])
            P.dma("sp", kiT[64:128, :], KI_d[:, :], kiT, R=[dr["KI"]], W=[kiT])
            vb = sb([128, 2048], F32, "vb", sD)
            P.dma("sp", vb[:], vbias_in[:, :], vb, W=[vb])
            cb = sb([128, 2, 256], F32, "cb", sD)
            P.dma("sp", cb[:], cbias_in[:, :, :], cb, W=[cb])
            score = sb([128, NT], F32, "score", sD)
            maskT = sb([128, NB, 256], U8, "maskT", sD)
            JW = 4096
            junk = sb([128, JW], U8, "junkc", sD)
            QTs = sb([128, 8, 256], BF16, "QTs", sD)
            QIs = sb([128, 8, 256], BF16, "QIs", sD)
            AGs = sb([128, 2, 1024], BF16, "AGs", sD)
            diag = sb([128, 16, 128], BF16, "diag", sD)
            Rr = [sb([128, 512], BF16, "R", sD) for _ in range(4)]
            mk = [sb([128, 512], BF16, "mk", sD) for _ in range(2)]
            stat = sb([128, 2, 40], F32, "stat", sD)
            bis = sb([128, 8], F32, "bis", sD)
            cparts = sb([128, 8], F32, "cparts", sD)
            Kr = [sb([128, 2048], BF16, "Kr", sD) for _ in range(2)]
            Vr = [sb([128, 16, 129], BF16, "Vr", sD) for _ in range(2)]
            for v_ in Vr:
                P.op("dve", lambda e, v_=v_: e.memset(v_[:, :, 128:129], 1.0), W=[v_])
            Pe = [sb([128, 256], BF16, "Pe", sD) for _ in range(3)]
            Pm = [sb([128, 256], BF16, "Pm", sD) for _ in range(3)]
            rec = sb([128, 2], F32, "rec", sD)
            oa = sb([128, 128], BF16, "oa", sD)
            oaT = [sb([128, 2, 128], BF16, "oaT", sD) for _ in range(2)]
            lgp = [ps([128, 512], F32, "lgp", sD) for _ in range(2)]
            scp = [ps([128, 512], F32, "scp", sD) for _ in range(1)]
            mtp = ps([128, 8, 128], BF16, "mtp", sD)
            stp = [ps([128, 512], F32, "stp", sD) for _ in range(2)]
            acc = [ps([128, 512], F32, "acc", sD) for _ in range(2)]
            lgp4 = [lgp[0], lgp[1], stp[0], stp[1]]
            ri = 0
            for g in range(min(G, KSEG)):
                N = 2048 * (g + 1) + 256
                nkt = (N + 511) // 512
                tok0 = g * 256
                P.dma("sp", QTs[:], QT_d[:, :, tok0:tok0 + 256].rearrange("h d t -> d h t"), QTs, R=[dr["QT"]], W=[QTs])
                P.dma("sp", QIs[:], QI_d[:, :, tok0:tok0 + 256].rearrange("h d t -> d h t"), QIs, R=[dr["QI"]], W=[QIs])
                P.dma("sp", AGs[:], AG_d[tok0:tok0 + 256, :].rearrange("(q p) n -> p q n", p=128), AGs, R=[dr["AG"]], W=[AGs])
                for qt in range(2 if KIDX else 0):
                    j = 2 * g + qt
                    for h in range(16):
                        P.op("dve", lambda e, h=h, j=j: e.tensor_scalar(out=diag[:, h, :], in0=identf[:], scalar1=wsc[:, j, h:h + 1],
                                                                        scalar2=None, op0=ALU.mult), R=[identf, wsc], W=[diag])
                    for kt in range(nkt):
                        k0 = kt * 512
                        kw = min(512, N - k0)
                        sp_ = scp[0]
                        def logits(h):
                            hp, hh = h // 2, h % 2
                            lg = lgp4[h % 4]
                            P.op("pe", lambda e, lg=lg, hp=hp, hh=hh, qt=qt, k0=k0, kw=kw: e.matmul(
                                lg[:, 0:kw], lhsT=QIs[hh * 64:(hh + 1) * 64, hp, qt * 128:(qt + 1) * 128],
                                rhs=kiT[hh * 64:(hh + 1) * 64, k0:k0 + kw], start=True, stop=True), R=[QIs, kiT], W=[lg])
                        logits(0)
                        logits(1)
                        for h in range(16):
                            lg = lgp4[h % 4]
                            if h + 2 < 16:
                                logits(h + 2)
                            r_ = Rr[ri % 4]; ri += 1
                            if h % 2 == 0:
                                P.op("act", lambda e, r_=r_, lg=lg, kw=kw: e.activation(out=r_[:, 0:kw], in_=lg[:, 0:kw], func=AF.Relu),
                                     R=[lg], W=[r_])
                            else:
                                P.op("dve", lambda e, r_=r_, lg=lg, kw=kw: e.tensor_scalar(out=r_[:, 0:kw], in0=lg[:, 0:kw], scalar1=0.0,
                                                                                      scalar2=None, op0=ALU.max), R=[lg], W=[r_])
                            P.op("pe", lambda e, sp_=sp_, h=h, r_=r_, kw=kw: e.matmul(
                                sp_[:, 0:kw], lhsT=diag[:, h, :], rhs=r_[:, 0:kw], start=(h == 0), stop=(h == 15)),
                                R=[diag, r_], W=[sp_])
                        P.op("dve", lambda e, sp_=sp_, kt=kt, kw=kw: e.tensor_reduce(out=stat[:, 0, kt:kt + 1], in_=sp_[:, 0:kw],
                                                                                 axis=AX.X, op=ALU.min), R=[sp_], W=[stat])
                        if kt < 4:
                            P.op("dve", lambda e, sp_=sp_, k0=k0, kw=kw: e.tensor_tensor(out=score[:, k0:k0 + kw], in0=sp_[:, 0:kw],
                                                                                     in1=vb[:, k0:k0 + kw], op=ALU.add),
                                 R=[sp_, vb], W=[score])
                        elif kt == nkt - 1:
                            P.op("dve", lambda e, sp_=sp_, k0=k0, kw=kw, qt=qt: e.tensor_tensor(
                                out=score[:, k0:k0 + kw], in0=sp_[:, 0:kw], in1=cb[:, qt, :], op=ALU.add), R=[sp_, cb], W=[score])
                        else:
                            P.op("dve", lambda e, sp_=sp_, k0=k0, kw=kw: e.tensor_copy(out=score[:, k0:k0 + kw], in_=sp_[:, 0:kw]),
                                 R=[sp_], W=[score])
                    P.op("dve", lambda e: e.tensor_reduce(out=bis[:, 1:2], in_=score[:, 0:N], axis=AX.X, op=ALU.max),
                         R=[score], W=[bis])
                    P.op("dve", lambda e, nkt=nkt: e.tensor_reduce(out=bis[:, 0:1], in_=stat[:, 0, 0:nkt], axis=AX.X, op=ALU.min),
                         R=[stat], W=[bis])
                    P.op("dve", lambda e: e.tensor_scalar(out=bis[:, 0:1], in0=bis[:, 0:1], scalar1=-1.0, scalar2=None, op0=ALU.add),
                         R=[bis], W=[bis])
                    P.op("dve", lambda e: e.tensor_scalar(out=bis[:, 1:2], in0=bis[:, 1:2], scalar1=1.0, scalar2=None, op0=ALU.add),
                         R=[bis], W=[bis])
                    nparts = (N + JW - 1) // JW
                    for it in range(NIT):
                        P.op("dve", lambda e: e.tensor_tensor(out=bis[:, 2:3], in0=bis[:, 0:1], in1=bis[:, 1:2], op=ALU.add),
                             R=[bis], W=[bis])
                        P.op("dve", lambda e: e.tensor_scalar(out=bis[:, 2:3], in0=bis[:, 2:3], scalar1=0.5, scalar2=None, op0=ALU.mult),
                             R=[bis], W=[bis])
                        for pi_ in range(nparts):
                            a0 = pi_ * JW
                            aw = min(JW, N - a0)
                            P.op("dve", lambda e, a0=a0, aw=aw, pi_=pi_: e.tensor_scalar(
                                out=junk[:, 0:aw], in0=score[:, a0:a0 + aw], scalar1=bis[:, 2:3], scalar2=None,
                                op0=ALU.is_ge, op1=ALU.add, accum_out=cparts[:, pi_:pi_ + 1]), R=[score, bis], W=[junk, cparts])
                        P.op("dve", lambda e, nparts=nparts: e.tensor_reduce(out=bis[:, 3:4], in_=cparts[:, 0:nparts], axis=AX.X, op=ALU.add),
                             R=[cparts], W=[bis])
                        P.op("dve", lambda e: e.tensor_scalar(out=bis[:, 4:5], in0=bis[:, 3:4], scalar1=TOPK - 0.5, scalar2=None,
                                                              op0=ALU.is_ge), R=[bis], W=[bis])
                        P.op("dve", lambda e: e.tensor_tensor(out=bis[:, 5:6], in0=bis[:, 2:3], in1=bis[:, 0:1], op=ALU.subtract),
                             R=[bis], W=[bis])
                        P.op("dve", lambda e: e.scalar_tensor_tensor(out=bis[:, 0:1], in0=bis[:, 5:6], scalar=bis[:, 4:5], in1=bis[:, 0:1],
                                                                     op0=ALU.mult, op1=ALU.add), R=[bis], W=[bis])
                        P.op("dve", lambda e: e.tensor_tensor(out=bis[:, 5:6], in0=bis[:, 1:2], in1=bis[:, 2:3], op=ALU.subtract),
                             R=[bis], W=[bis])
                        P.op("dve", lambda e: e.scalar_tensor_tensor(out=bis[:, 1:2], in0=bis[:, 5:6], scalar=bis[:, 4:5], in1=bis[:, 2:3],
                                                                     op0=ALU.mult, op1=ALU.add), R=[bis], W=[bis])
                    for kt in range(nkt):
                        k0 = kt * 512
                        kw = min(512, N - k0)
                        m_ = mk[kt % 2]
                        P.op("dve", lambda e, m_=m_, k0=k0, kw=kw: e.tensor_scalar(out=m_[:, 0:kw], in0=score[:, k0:k0 + kw],
                                                                               scalar1=bis[:, 0:1], scalar2=None, op0=ALU.is_ge),
                             R=[score, bis], W=[m_])
                        nb4 = kw // 128
                        for b4 in range(nb4):
                            P.op("pe", lambda e, m_=m_, b4=b4: e.transpose(mtp[:, b4, :], m_[:, b4 * 128:(b4 + 1) * 128], identb[:]),
                                 R=[m_, identb], W=[mtp])
                        P.op("act", lambda e, kt=kt, nb4=nb4, qt=qt: e.copy(out=maskT[:, kt * 4:kt * 4 + nb4, qt * 128:(qt + 1) * 128],
                                                                        in_=mtp[:, 0:nb4, :]), R=[mtp], W=[maskT])
                nblk = N // 128
                for hd in range(8 if KATT else 0):
                    bidx = 0
                    for ch in range(g + 2):
                        nbc = 16 if ch < g + 1 else 2
                        K_ = Kr[(hd * (g + 2) + ch) % 2]
                        V_ = Vr[(hd * (g + 2) + ch) % 2]
                        P.dma("sp", K_[:, 0:nbc * 128], KT_d[hd, :, ch * 2048:ch * 2048 + nbc * 128], K_, R=[dr["KT"]], W=[K_])
                        P.dma("sp", V_[:, 0:nbc, 0:128], V_d[hd, ch, :, 0:nbc * 128].rearrange("p (b d) -> p b d", d=128), V_,
                              R=[dr["V"]], W=[V_])
                        def smat(b_):
                            kb = ch * 16 + b_
                            st_ = stp[kb % 2]
                            P.op("pe", lambda e, st_=st_, K_=K_, b_=b_, hd=hd: e.matmul(
                                st_[:, 0:256], lhsT=K_[:, b_ * 128:(b_ + 1) * 128], rhs=QTs[:, hd, :], start=True, stop=True),
                                R=[K_, QTs], W=[st_])
                        smat(0)
                        for b_ in range(nbc):
                            kb = ch * 16 + b_
                            st_ = stp[kb % 2]
                            pe_ = Pe[kb % 3]
                            pm_ = Pm[kb % 3]
                            P.op("act", lambda e, pe_=pe_, st_=st_: e.activation(out=pe_[:], in_=st_[:, 0:256], func=AF.Exp,
                                                                               scale=float(128 ** -0.5)), R=[st_], W=[pe_])
                            P.op("dve", lambda e, pm_=pm_, pe_=pe_, kb=kb: e.tensor_tensor(out=pm_[:], in0=pe_[:], in1=maskT[:, kb, :],
                                                                                        op=ALU.mult), R=[pe_, maskT], W=[pm_])
                            if b_ + 1 < nbc:
                                smat(b_ + 1)
                            for qt in range(2):
                                P.op("pe", lambda e, qt=qt, pm_=pm_, V_=V_, b_=b_, kb=kb, nblk=nblk: e.matmul(
                                    acc[qt][:, 0:129], lhsT=pm_[:, qt * 128:(qt + 1) * 128], rhs=V_[:, b_, :],
                                    start=(kb == 0), stop=(kb == nblk - 1)), R=[pm_, V_], W=[acc[qt]])
                    o2 = oaT[hd % 2]
                    for qt in range(2):
                        P.op("dve", lambda e, qt=qt: e.reciprocal(out=rec[:, qt:qt + 1], in_=acc[qt][:, 128:129]), R=[acc[qt]], W=[rec])
                        P.op("dve", lambda e, qt=qt, hd=hd: e.scalar_tensor_tensor(
                            out=oa[:], in0=acc[qt][:, 0:128], scalar=rec[:, qt:qt + 1], in1=AGs[:, qt, hd * 128:(hd + 1) * 128],
                            op0=ALU.mult, op1=ALU.mult), R=[acc[qt], rec, AGs], W=[oa])
                        P.op("pe", lambda e, qt=qt: e.transpose(mtp[:, qt, :], oa[:], identb[:]), R=[oa, identb], W=[mtp])
                    P.op("act", lambda e, o2=o2: e.copy(out=o2[:], in_=mtp[:, 0:2, :]), R=[mtp], W=[o2])
                    P.dma("pool", CT_d[hd, :, tok0:tok0 + 256], o2[:].rearrange("p q t -> p (q t)"), o2, R=[o2], W=[dr["CT"]])

        flush(); sD.close()
        sE = ExitStack()
        if STOP >= 5:
            Wo = sb([128, KC, D], BF16, "Wo", sE)
            stg = [sb([128, 1024], F32, "stg", sE) for _ in range(3)]
            ci = 0
            for kc in range(KC):
                for hf in range(2):
                    s = stg[ci % 3]
                    P.dma("sp", s[:], w_out[kc * 128:(kc + 1) * 128, hf * 1024:(hf + 1) * 1024], s, W=[s])
                    en = ("pool", "dve", "act")[ci % 3]
                    if en == "act":
                        P.op("act", lambda e, s=s, kc=kc, hf=hf: e.copy(out=Wo[:, kc, hf * 1024:(hf + 1) * 1024], in_=s[:]),
                             R=[s], W=[Wo])
                    else:
                        P.op(en, lambda e, s=s, kc=kc, hf=hf: e.tensor_copy(out=Wo[:, kc, hf * 1024:(hf + 1) * 1024], in_=s[:]),
                             R=[s], W=[Wo])
                    ci += 1
            gbc = sb([128, D], F32, "gbc", sE)
            gp = ps([128, 512], F32, "gp", sE)
            for nb in range(4):
                P.op("pe", lambda e, nb=nb: e.matmul(gp[:, :], lhsT=ones1[0:1, :], rhs=grow[0:1, nb * 512:(nb + 1) * 512],
                                                     start=True, stop=True), R=[ones1, grow], W=[gp])
                P.op("act", lambda e, nb=nb: e.copy(out=gbc[:, nb * 512:(nb + 1) * 512], in_=gp[:, :]), R=[gp], W=[gbc])
            CTs = [sb([128, KC, 128], BF16, "CTs", sE) for _ in range(2)]
            xo = [sb([128, D], F32, "xo", sE) for _ in range(2)]
            yo = [sb([128, D], F32, "yo", sE) for _ in range(2)]
            mp = [ps([128, 512], F32, "mp", sE) for _ in range(3)]
            mi = 0
            for j in range(NO):
                T_ = 16 + 16 * (j // 2) + (j % 2)
                c_ = CTs[j % 2]
                x_ = xo[j % 2]
                y_ = yo[j % 2]
                P.dma("sp", c_[:], CT_d[:, :, j * 128:(j + 1) * 128].rearrange("k p t -> p k t"), c_, R=[dr["CT"]], W=[c_])
                P.dma("sp", x_[:], xa[T_ * 128:(T_ + 1) * 128, :], x_, W=[x_])
                for nb in range(4):
                    m_ = mp[mi % 3]; mi += 1
                    for kc in range(KC):
                        P.op("pe", lambda e, m_=m_, c_=c_, kc=kc, nb=nb: e.matmul(
                            m_[:, :], lhsT=c_[:, kc, :], rhs=Wo[:, kc, nb * 512:(nb + 1) * 512],
                            start=(kc == 0), stop=(kc == KC - 1)), R=[c_, Wo], W=[m_])
                    P.op("dve", lambda e, m_=m_, y_=y_, nb=nb: e.tensor_tensor(out=y_[:, nb * 512:(nb + 1) * 512], in0=m_[:, :],
                                                                            in1=gbc[:, nb * 512:(nb + 1) * 512], op=ALU.mult),
                         R=[m_, gbc], W=[y_])
                    P.op("pool", lambda e, x_=x_, y_=y_, nb=nb: e.tensor_tensor(out=y_[:, nb * 512:(nb + 1) * 512],
                                                                             in0=y_[:, nb * 512:(nb + 1) * 512],
                                                                             in1=x_[:, nb * 512:(nb + 1) * 512], op=ALU.add),
                         R=[x_, y_], W=[y_])
                P.dma("pool", y[j * 128:(j + 1) * 128, :], y_[:], y_, R=[y_], W=[dr["y"]])
            P.wait_all("pool", yo)
            P.wait_all("sp", yo)

        flush(); sE.close()
    return nc


_CACHE = {}


def _consts():
    p = np.arange(128)
    ch = p // 64
    same = ch[:, None] == ch[None, :]
    tri = (p[:, None] <= p[None, :]) & same
    mid = (p[:, None] <= (ch[None, :] * 64 + 31)) & same
    last = same
    md1 = tri.astype(np.float32) - mid.astype(np.float32)
    md2 = last.astype(np.float32) - tri.astype(np.float32)
    md3 = tri.astype(np.float32)
    cind = np.stack([(ch == 0), (ch == 1)], axis=1).astype(np.float32)
    cmask = tri.astype(np.float32)
    half = 64
    invf = (10000.0 ** (-np.arange(half, dtype=np.float32) / half)).astype(np.float32)
    cb = np.zeros((128, 2, 256), np.float32)
    for qt in range(2):
        qc = (qt * 128 + p) // 64
        kc = np.arange(256) // 64
        cb[:, qt, :] = np.where(kc[None, :] <= qc[:, None], 0.0, -BIG)
    return dict(md1=md1, md2=md2, md3=md3, cind=cind, cmask=cmask,
                invf=np.tile(invf[None, :], (128, 1)).astype(np.float32),
                identf=np.eye(128, dtype=np.float32), cbias=cb)


def kernel(x, c, positions, ada_w, ada_b, norm_g, w_in, q_norm_g, k_norm_g, idx_k_norm_g,
           hgrn_lb_logits, hgrn_norm_g, w_out):
    x = np.asarray(x, np.float32)[0]
    S = x.shape[0]
    G = S // 2048
    NT = 2048 * G + 256
    NTL = NT // 128
    if G not in _CACHE:
        _CACHE[G] = build(G)
    nc = _CACHE[G]
    cst = _consts()
    pos = np.asarray(positions, np.int32)[0]
    ada_b = np.asarray(ada_b, np.float32)[0]
    common = dict(
        cfm=np.ascontiguousarray(np.asarray(c, np.float32)[0].reshape(KC, 128).T),
        ada_w=np.ascontiguousarray(np.asarray(ada_w, np.float32)[0]),
        adab_fm=np.ascontiguousarray(ada_b.reshape(48, 128).T),
        adab_g=np.ascontiguousarray(ada_b[2 * D:3 * D][None, :]),
        ng_fm=np.ascontiguousarray(np.asarray(norm_g, np.float32)[0].reshape(KC, 128).T),
        w_in=np.ascontiguousarray(np.asarray(w_in, np.float32)[0]),
        w_out=np.ascontiguousarray(np.asarray(w_out, np.float32)[0]),
        gq=np.tile(np.asarray(q_norm_g, np.float32)[0][None, :], (128, 1)),
        gk=np.tile(np.asarray(k_norm_g, np.float32)[0][None, :], (128, 1)),
        gi=np.tile(np.asarray(idx_k_norm_g, np.float32)[0][None, :], (128, 1)),
        hg=np.tile(np.asarray(hgrn_norm_g, np.float32)[0][None, :], (128, 1)),
        lbl=np.ascontiguousarray(np.tile(np.asarray(hgrn_lb_logits, np.float32)[:, None, :], (1, 128, 1))),
        **cst,
    )
    in_maps = []
    for cc in range(8):
        npad = 2048 - 256 * cc
        nreal = NT - npad
        xa = np.zeros((NT, D), np.float32)
        xa[npad:] = x[:nreal]
        pa = np.zeros((NT,), np.int32)
        pa[npad:] = pos[:nreal]
        va = np.zeros((NT,), np.float32)
        va[npad:] = 1.0
        vbias = np.where(va[:2048] > 0, 0.0, -BIG).astype(np.float32)
        m = dict(common)
        m.update(xa=xa, posa=np.ascontiguousarray(pa.reshape(NTL, 128).T),
                 valid=np.ascontiguousarray(va.reshape(NTL, 128).T),
                 vbias=np.tile(vbias[None, :], (128, 1)))
        in_maps.append(m)
    res = run_bass_kernel_spmd(nc, in_maps, core_ids=list(range(8)))
    out = np.zeros((1, S, D), np.float32)
    for cc in range(8):
        yc = res.results[cc]["y"]
        for g in range(G):
            t0 = 2048 * g + 256 * cc
            out[0, t0:t0 + 256] = yc[g * 256:(g + 1) * 256]
    return out
```
